# Optimizing a Trainium2 kernel written in Bass

```python
import math
import jax, jax.numpy as jnp
from jax import lax
import numpy as np

D_MODEL = 1024
BATCH = 8
SEQ = 2048
DEPTH = 2
DEC_BATCH = 4
DEC_SEQ = 4096
PAST_LEN = 128

GRID_W = 64
MLA_HEADS = 8
MLA_NOPE = 64
MLA_ROPE = 32
MLA_V = 64
Q_LORA = 384
KV_LORA = 256
ROPE_BASE = 10000.0
Q_BLOCK = 128
NA_HEADS = 8
NA_HD = 64
NA_KH = 8
NA_KW = 16
D_RNN = 1024
LRU_BLOCKS = 8
LRU_BW = D_RNN // LRU_BLOCKS
REC_CONV = 4
LRU_C = 8.0
D_FF = 4096
FFN_CONV = 3
EPS = 1e-6

ATTN_IN = Q_LORA + KV_LORA + MLA_ROPE + 3 * NA_HEADS * NA_HD
ATTN_OUT = MLA_HEADS * MLA_V + NA_HEADS * NA_HD
N_EVEN = (DEPTH + 1) // 2
N_ODD = DEPTH // 2

kernel_name = 'hybrid_mla_natten_rglru_encoder'


def rmsnorm(x, g):
    xf = x.astype(jnp.float32)
    y = xf * lax.rsqrt(jnp.mean(xf * xf, axis=-1, keepdims=True) + EPS)
    return (y * g.astype(jnp.float32)).astype(x.dtype)


def dwconv(x, w, b):
    K = w.shape[0]
    S = x.shape[1]
    left = (K - 1) // 2
    xp = jnp.pad(x, ((0, 0), (left, K - 1 - left), (0, 0)))
    y = w[0] * xp[:, 0:S]
    for k in range(1, K):
        y = y + w[k] * xp[:, k:k + S]
    return y + b


def rope(x, cos, sin):
    half = x.shape[-1] // 2
    x1, x2 = x[..., :half], x[..., half:]
    cos = cos.astype(x.dtype)
    sin = sin.astype(x.dtype)
    return jnp.concatenate([x1 * cos - x2 * sin, x2 * cos + x1 * sin], axis=-1)


def mla(q_lat, kv_lat, k_pe_raw, q_norm, w_uq, kv_norm, w_ukv):
    B, S, _ = q_lat.shape
    q = (rmsnorm(q_lat, q_norm) @ w_uq).reshape(B, S, MLA_HEADS, MLA_NOPE + MLA_ROPE)
    kv = (rmsnorm(kv_lat, kv_norm) @ w_ukv).reshape(B, S, MLA_HEADS, MLA_NOPE + MLA_V)
    q_nope, q_pe = q[..., :MLA_NOPE], q[..., MLA_NOPE:]
    k_nope, v = kv[..., :MLA_NOPE], kv[..., MLA_NOPE:]
    pos = jnp.arange(S, dtype=jnp.float32)
    inv = ROPE_BASE ** (-jnp.arange(0, MLA_ROPE, 2, dtype=jnp.float32) / MLA_ROPE)
    ang = pos[:, None] * inv[None, :]
    cos, sin = jnp.cos(ang), jnp.sin(ang)
    q_pe = rope(q_pe, cos[:, None, :], sin[:, None, :])
    k_pe = rope(k_pe_raw, cos, sin)
    scale = (MLA_NOPE + MLA_ROPE) ** -0.5
    nqb = S // Q_BLOCK

    def q_block(xq):
        qn, qp = xq
        s = (jnp.einsum('bqhd,bkhd->bhqk', qn, k_nope)
             + jnp.einsum('bqhr,bkr->bhqk', qp, k_pe))
        p = jax.nn.softmax(s.astype(jnp.float32) * scale, axis=-1).astype(v.dtype)
        return jnp.einsum('bhqk,bkhv->bqhv', p, v)

    def to_blocks(t):
        return jnp.moveaxis(t.reshape(B, nqb, Q_BLOCK, *t.shape[2:]), 1, 0)

    o = lax.map(q_block, (to_blocks(q_nope), to_blocks(q_pe)))
    return jnp.moveaxis(o, 0, 1).reshape(B, S, MLA_HEADS * MLA_V)


def neighborhood_attn(q, k, v, rpb):
    B, S, H, d = q.shape
    rows = S // GRID_W
    kh = min(NA_KH, rows)
    qg = q.reshape(B, rows, GRID_W, H, d)
    kg = k.reshape(B, rows, GRID_W, H, d)
    vg = v.reshape(B, rows, GRID_W, H, d)
    cols = np.arange(GRID_W)
    col_start = np.clip(cols - NA_KW // 2, 0, GRID_W - NA_KW)
    col_idx = col_start[:, None] + np.arange(NA_KW)[None, :]
    col_bias_idx = col_idx - cols[:, None] + (NA_KW - 1)
    scale = d ** -0.5

    def row_block(r):
        rs = jnp.clip(r - kh // 2, 0, rows - kh)
        q_row = lax.dynamic_index_in_dim(qg, r, axis=1, keepdims=False)
        k_win = lax.dynamic_slice_in_dim(kg, rs, kh, axis=1)[:, :, col_idx]
        v_win = lax.dynamic_slice_in_dim(vg, rs, kh, axis=1)[:, :, col_idx]
        row_bias_idx = rs + jnp.arange(kh) - r + (NA_KH - 1)
        bias = rpb[:, row_bias_idx[None, :, None], col_bias_idx[:, None, :]]
        s = (jnp.einsum('bqhd,bkqwhd->bhqkw', q_row, k_win).astype(jnp.float32) * scale
             + bias.astype(jnp.float32)[None])
        p = jax.nn.softmax(s.reshape(B, H, GRID_W, kh * NA_KW), axis=-1)
        p = p.reshape(s.shape).astype(v.dtype)
        return jnp.einsum('bhqkw,bkqwhd->bqhd', p, v_win)

    o = lax.map(row_block, jnp.arange(rows))
    return jnp.moveaxis(o, 0, 1).reshape(B, S, H * d)


def attn_mixer(h, w_in, q_norm, w_uq, kv_norm, w_ukv, rpb, w_out):
    B, S, _ = h.shape
    z = h @ w_in
    q_lat, kv_lat, k_pe, na_qkv = jnp.split(
        z, [Q_LORA, Q_LORA + KV_LORA, Q_LORA + KV_LORA + MLA_ROPE], axis=-1)
    na_qkv = na_qkv.reshape(B, S, 3, NA_HEADS, NA_HD)
    o_mla = mla(q_lat, kv_lat, k_pe, q_norm, w_uq, kv_norm, w_ukv)
    o_na = neighborhood_attn(na_qkv[:, :, 0], na_qkv[:, :, 1], na_qkv[:, :, 2], rpb)
    return jnp.concatenate([o_mla, o_na], axis=-1) @ w_out


def _lin_combine(left, right):
    a_l, b_l = left
    a_r, b_r = right
    return a_r * a_l, a_r * b_l + b_r


def rglru_mixer(h, w_in, conv_w, conv_b, ga_w, ga_b, gx_w, gx_b, lam, w_out):
    B, S, _ = h.shape
    gate, xb = jnp.split(h @ w_in, 2, axis=-1)
    xb = dwconv(xb, conv_w, conv_b)
    xr = xb.reshape(B, S, LRU_BLOCKS, LRU_BW)
    r = jax.nn.sigmoid((jnp.einsum('bsnc,encd->ebsnd', xr, ga_w).reshape(2, B, S, D_RNN)
                        + ga_b[:, None, None, :]).astype(jnp.float32))
    i = jax.nn.sigmoid((jnp.einsum('bsnc,encd->ebsnd', xr, gx_w).reshape(2, B, S, D_RNN)
                        + gx_b[:, None, None, :]).astype(jnp.float32))
    log_a = -LRU_C * r * jax.nn.softplus(-lam.astype(jnp.float32))[:, None, None, :]
    a = jnp.exp(log_a)
    b = jnp.sqrt(-jnp.expm1(2.0 * log_a)) * (i * xb.astype(jnp.float32)[None])
    h_fwd = lax.associative_scan(_lin_combine, (a[0], b[0]), axis=1)[1]
    h_bwd = lax.associative_scan(_lin_combine, (a[1], b[1]), axis=1, reverse=True)[1]
    y = (h_fwd + h_bwd).astype(h.dtype) * jax.nn.gelu(gate, approximate=True)
    return y @ w_out


def conv_ffn(h, w_up, conv_w, conv_b, w_down):
    u = dwconv(h @ w_up, conv_w, conv_b)
    g, val = jnp.split(u, 2, axis=-1)
    return (jax.nn.gelu(g, approximate=True) * val) @ w_down


def _trunk(x, norm_mix_pre, norm_mix_post, norm_ffn_pre, norm_ffn_post,
           w_in_attn, q_norm, w_uq, kv_norm, w_ukv, na_rpb, w_out_attn,
           w_in_rec, conv_w_rec, conv_b_rec, gate_a_w, gate_a_b, gate_x_w, gate_x_b,
           lru_lambda, w_out_rec, w_ffn_up, conv_w_ffn, conv_b_ffn, w_ffn_down):
    for li in range(DEPTH):
        j = li // 2
        h = rmsnorm(x, norm_mix_pre[li])
        if li % 2 == 0:
            m = attn_mixer(h, w_in_attn[j], q_norm[j], w_uq[j], kv_norm[j], w_ukv[j],
                           na_rpb[j], w_out_attn[j])
        else:
            m = rglru_mixer(h, w_in_rec[j], conv_w_rec[j], conv_b_rec[j], gate_a_w[j],
                            gate_a_b[j], gate_x_w[j], gate_x_b[j], lru_lambda[j], w_out_rec[j])
        x = x + rmsnorm(m, norm_mix_post[li])
        h = rmsnorm(x, norm_ffn_pre[li])
        f = conv_ffn(h, w_ffn_up[li], conv_w_ffn[li], conv_b_ffn[li], w_ffn_down[li])
        x = x + rmsnorm(f, norm_ffn_post[li])
    return x


def setup_inputs(seed: int = 0) -> dict:
    key = jax.random.key(seed)
    ks = jax.random.split(key, 26)
    f32 = jnp.float32

    def nrm(k, shape, s):
        return jax.random.normal(k, shape, f32) * s

    def gain(k, shape):
        return 1.0 + 0.05 * jax.random.normal(k, shape, f32)

    a0 = jax.random.uniform(ks[20], (N_ODD, 2, D_RNN), f32, minval=0.9, maxval=0.999)
    p = a0 ** (1.0 / LRU_C)
    lru_lambda = jnp.log(p) - jnp.log1p(-p)
    return {
        'x_prompt': nrm(ks[0], (BATCH, SEQ, D_MODEL), 1.0),
        'x_sample': nrm(ks[1], (DEC_BATCH, DEC_SEQ, D_MODEL), 1.0),
        'norm_mix_pre': gain(ks[2], (DEPTH, D_MODEL)),
        'norm_mix_post': gain(ks[3], (DEPTH, D_MODEL)),
        'norm_ffn_pre': gain(ks[4], (DEPTH, D_MODEL)),
        'norm_ffn_post': gain(ks[5], (DEPTH, D_MODEL)),
        'w_in_attn': nrm(ks[6], (N_EVEN, D_MODEL, ATTN_IN), D_MODEL ** -0.5),
        'q_norm': gain(ks[7], (N_EVEN, Q_LORA)),
        'w_uq': nrm(ks[8], (N_EVEN, Q_LORA, MLA_HEADS * (MLA_NOPE + MLA_ROPE)), Q_LORA ** -0.5),
        'kv_norm': gain(ks[9], (N_EVEN, KV_LORA)),
        'w_ukv': nrm(ks[10], (N_EVEN, KV_LORA, MLA_HEADS * (MLA_NOPE + MLA_V)), KV_LORA ** -0.5),
        'na_rpb': nrm(ks[11], (N_EVEN, NA_HEADS, 2 * NA_KH - 1, 2 * NA_KW - 1), 0.1),
        'w_out_attn': nrm(ks[12], (N_EVEN, ATTN_OUT, D_MODEL), ATTN_OUT ** -0.5),
        'w_in_rec': nrm(ks[13], (N_ODD, D_MODEL, 2 * D_RNN), D_MODEL ** -0.5),
        'conv_w_rec': nrm(ks[14], (N_ODD, REC_CONV, D_RNN), REC_CONV ** -0.5),
        'conv_b_rec': nrm(ks[15], (N_ODD, D_RNN), 0.01),
        'gate_a_w': nrm(ks[16], (N_ODD, 2, LRU_BLOCKS, LRU_BW, LRU_BW), LRU_BW ** -0.5),
        'gate_a_b': nrm(ks[17], (N_ODD, 2, D_RNN), 0.01),
        'gate_x_w': nrm(ks[18], (N_ODD, 2, LRU_BLOCKS, LRU_BW, LRU_BW), LRU_BW ** -0.5),
        'gate_x_b': nrm(ks[19], (N_ODD, 2, D_RNN), 0.01),
        'lru_lambda': lru_lambda,
        'w_out_rec': nrm(ks[21], (N_ODD, D_RNN, D_MODEL), D_RNN ** -0.5),
        'w_ffn_up': nrm(ks[22], (DEPTH, D_MODEL, 2 * D_FF), D_MODEL ** -0.5),
        'conv_w_ffn': nrm(ks[23], (DEPTH, FFN_CONV, 2 * D_FF), FFN_CONV ** -0.5),
        'conv_b_ffn': nrm(ks[24], (DEPTH, 2 * D_FF), 0.01),
        'w_ffn_down': nrm(ks[25], (DEPTH, D_FF, D_MODEL), D_FF ** -0.5),
    }


def reference(x_prompt, x_sample, norm_mix_pre, norm_mix_post, norm_ffn_pre, norm_ffn_post,
              w_in_attn, q_norm, w_uq, kv_norm, w_ukv, na_rpb, w_out_attn,
              w_in_rec, conv_w_rec, conv_b_rec, gate_a_w, gate_a_b, gate_x_w, gate_x_b,
              lru_lambda, w_out_rec, w_ffn_up, conv_w_ffn, conv_b_ffn, w_ffn_down):
    weights = (norm_mix_pre, norm_mix_post, norm_ffn_pre, norm_ffn_post,
               w_in_attn, q_norm, w_uq, kv_norm, w_ukv, na_rpb, w_out_attn,
               w_in_rec, conv_w_rec, conv_b_rec, gate_a_w, gate_a_b, gate_x_w, gate_x_b,
               lru_lambda, w_out_rec, w_ffn_up, conv_w_ffn, conv_b_ffn, w_ffn_down)
    y_prompt = _trunk(x_prompt, *weights)
    y_sample = _trunk(x_sample, *weights)
    return (y_prompt, y_sample)
```

```python
import numpy as np
import concourse.bass as bass
import concourse.mybir as mybir
from concourse.bass_utils import run_bass_kernel_spmd

F32 = mybir.dt.float32
BF16 = mybir.dt.bfloat16
ALU = mybir.AluOpType
AF = mybir.ActivationFunctionType

SAME_ENGINE_SYNC = True
SEM_LIMIT = 60000
DMA_RING = 12


class Dep:
    __slots__ = ("eng", "sem", "val")

    def __init__(self, eng, sem, val):
        self.eng = eng
        self.sem = sem
        self.val = val


class EngState:
    def __init__(self, name):
        self.name = name
        self.ops = []
        self.sem = None
        self.count = 0
        self.pending = []
        self.waited = {}
        self.ring = []
        self.ring_pos = 0
        self.last = None


class Prog:
    ENGS = ("pe", "act", "dve", "pool", "sp")

    def __init__(self, nc, nsems=100):
        self.nc = nc
        self.sem_pool = [nc.alloc_semaphore(name=f"s{i}") for i in range(nsems)]
        self.sem_idx = 0
        self.E = {n: EngState(n) for n in self.ENGS}
        for e in self.E.values():
            e.sem = self.new_sem()
        for q in ("sp", "pool", "act"):
            self.E[q].ring = [[self.new_sem(), 0] for _ in range(DMA_RING)]
        self.res = {}
        self.dma_deps_outstanding = []

    def new_sem(self):
        s = self.sem_pool[self.sem_idx]
        self.sem_idx += 1
        return s

    def _collect(self, eng, reads, writes):
        deps = []
        for r in reads:
            st = self.res.get(r)
            if st and st[0] is not None:
                deps.append(st[0])
        for w in writes:
            st = self.res.get(w)
            if st:
                if st[0] is not None:
                    deps.append(st[0])
                deps.extend(st[1])
        return deps

    def _waits_for(self, e, deps, is_dma=False):
        need = {}
        for d in deps:
            if d.eng == e.name and not isinstance(d, DmaDep) and not is_dma:
                if e.name == "pe" or not SAME_ENGINE_SYNC:
                    continue
            assert d.val is not None, f"dependency on pending (non-inc) op of {d.eng}"
            k = id(d.sem)
            if k not in need or need[k][1] < d.val:
                need[k] = (d.sem, d.val)
        waits = []
        for k, (sem, val) in need.items():
            if e.waited.get(k, 0) >= val:
                continue
            e.waited[k] = val
            waits.append((sem, val))
        return waits

    def _mark(self, dep, reads, writes):
        for r in reads:
            st = self.res.setdefault(r, [None, []])
            st[1].append(dep)
        for w in writes:
            self.res[w] = [dep, []]

    def op(self, eng, fn, reads=(), writes=(), inc=True):
        e = self.E[eng]
        deps = self._collect(eng, reads, writes)
        waits = self._waits_for(e, deps)
        if inc:
            if e.count >= SEM_LIMIT:
                e.sem = self.new_sem()
                e.count = 0
            e.count += 1
            dep = Dep(eng, e.sem, e.count)
            for p in e.pending:
                p.sem = e.sem
                p.val = e.count
            e.pending = []
            e.ops.append((waits, fn, e.sem, 1))
        else:
            dep = Dep(eng, None, None)
            e.pending.append(dep)
            e.ops.append((waits, fn, None, 0))
        e.last = dep
        self._mark(dep, reads, writes)
        return dep

    def dma(self, q, fn, reads=(), writes=()):
        e = self.E[q]
        deps = self._collect(q, reads, writes)
        slot = e.ring[e.ring_pos % DMA_RING]
        e.ring_pos += 1
        if slot[1] >= SEM_LIMIT - 16:
            slot[0] = self.new_sem()
            slot[1] = 0
            prev = None
        else:
            prev = DmaDep(q, slot[0], slot[1]) if slot[1] > 0 else None
        if prev is not None:
            deps.append(prev)
        waits = self._waits_for(e, deps, is_dma=True)
        slot[1] += 16
        dep = DmaDep(q, slot[0], slot[1])
        e.ops.append((waits, fn, slot[0], 16))
        self._mark(dep, reads, writes)
        self.dma_deps_outstanding.append(dep)
        return dep

    def barrier(self):
        alld = []
        for e in self.E.values():
            assert not e.pending, f"pending non-inc ops on {e.name} at barrier"
            if e.count > 0:
                alld.append(Dep(e.name + "_b", e.sem, e.count))
            if e.name == "pool":
                continue
            for slot in e.ring:
                if slot[1] > 0:
                    alld.append(DmaDep(e.name + "_b", slot[0], slot[1]))
        for e in self.E.values():
            waits = self._waits_for(e, alld)
            if waits:
                e.ops.append((waits, None, None, 0))
        keep = {}
        for k, st in self.res.items():
            lw = st[0] if (isinstance(st[0], DmaDep) and st[0].eng == "pool") else None
            rd = [d for d in st[1] if isinstance(d, DmaDep) and d.eng == "pool"]
            if lw is not None or rd:
                keep[k] = [lw, rd]
        self.res = keep

    def emit(self):
        nc = self.nc
        E = self.E

        def run(engobj, st):
            for waits, fn, sem, amt in st.ops:
                for (s, v) in waits:
                    engobj.wait_ge(s, v)
                if fn is not None:
                    inst = fn(engobj)
                    if sem is not None:
                        inst.then_inc(sem, amt)

        with nc.Block() as block:
            @block.tensor
            def _(x):
                run(x, E["pe"])

            @block.scalar
            def _(x):
                run(x, E["act"])

            @block.vector
            def _(x):
                run(x, E["dve"])

            @block.gpsimd
            def _(x):
                run(x, E["pool"])

            @block.sync
            def _(x):
                run(x, E["sp"])

    def stats(self):
        return {n: len(e.ops) for n, e in self.E.items()}, self.sem_idx


class DmaDep(Dep):
    __slots__ = ()


class Arena:
    def __init__(self, ap_f32, nbytes):
        self.base = ap_f32
        self.n = nbytes
        self.off = 0
        self.marks = []

    def push(self):
        self.marks.append(self.off)

    def pop(self):
        self.off = self.marks.pop()

    def alloc(self, free_shape, dtype, parts=128):
        esz = 4 if dtype == F32 else 2
        nel = int(np.prod(free_shape))
        nb = (nel * esz + 63) // 64 * 64
        assert self.off + nb <= self.n, f"arena overflow {self.off}+{nb}>{self.n}"
        a = self.base[:, self.off // 4:(self.off + nb) // 4]
        self.off += nb
        if dtype != F32:
            a = a.bitcast(dtype)
        a = a[:, 0:nel]
        if len(free_shape) == 2:
            a = a.rearrange("p (a b) -> p a b", a=free_shape[0])
        elif len(free_shape) == 3:
            a = a.rearrange("p (a b c) -> p a b c", a=free_shape[0], b=free_shape[1])
        elif len(free_shape) == 4:
            a = a.rearrange("p (a b c d) -> p a b c d", a=free_shape[0], b=free_shape[1], c=free_shape[2])
        return a


def I(method, *args, **kw):
    def fn(e):
        return getattr(e, method)(*args, **kw)
    return fn


NT = 4096
D = 1024
A2_HEADS = 8
A2_QB = 8
A2_S = 7
NEG = -30000.0
EPS = 1e-6
FT_TILES = [410, 410, 410, 410, 408] * 2
ARENA_BYTES = 206 * 1024

PV = {}
_pvo = 0


def _pv_add(name, w):
    global _pvo
    PV[name] = (_pvo, w)
    _pvo += w


for _n, _w in [("g_mix_pre", 16), ("g_mix_post", 16), ("g_ffn_pre", 16), ("g_ffn_post", 16), ("q_norm", 3), ("kv_norm", 2),
               ("rc_w", 32), ("rc_b", 8), ("ga_b", 16), ("gx_b", 16), ("lam", 16), ("fc_w", 384), ("fc_b", 128),
               ("keep", 1), ("eps", 1), ("one", 1), ("quarter", 1)]:
    _pv_add(_n, _w)
NPV = _pvo

WSHAPES = {
    "w_in": [128, 8 * 2368], "w_uq": [128, 3 * 8 * 192], "w_ukv": [128, 2 * 1024], "w_oa": [128, 8 * 1024],
    "w_ir": [128, 8 * 2048], "w_or": [128, 8 * 1024], "w_g": [128, 2 * 2 * 8 * 128],
    "w_up": [2 * 8 * 128, 8192], "w_dn": [2 * 8 * 128, 4096],
}


def cv(vec):
    return np.ascontiguousarray(np.asarray(vec, np.float32).reshape(-1, 128).T)


def host_prep(inp):
    f = lambda k: np.asarray(inp[k], np.float32)
    pvb = np.zeros((128, NPV), np.float32)

    def put(name, arr):
        o, w = PV[name]
        assert arr.shape == (128, w), (name, arr.shape, w)
        pvb[:, o:o + w] = arr

    for nm, key in [("g_mix_pre", "norm_mix_pre"), ("g_mix_post", "norm_mix_post"), ("g_ffn_pre", "norm_ffn_pre"), ("g_ffn_post", "norm_ffn_post")]:
        put(nm, np.concatenate([cv(f(key)[l]) for l in range(2)], axis=1))
    put("q_norm", cv(f("q_norm")[0]))
    put("kv_norm", cv(f("kv_norm")[0]))
    put("rc_w", np.concatenate([cv(f("conv_w_rec")[0, k]) for k in range(4)], axis=1))
    put("rc_b", cv(f("conv_b_rec")[0]))
    put("ga_b", np.concatenate([cv(f("gate_a_b")[0, e]) for e in range(2)], axis=1))
    put("gx_b", np.concatenate([cv(f("gate_x_b")[0, e]) for e in range(2)], axis=1))
    put("lam", np.concatenate([cv(f("lru_lambda")[0, e]) for e in range(2)], axis=1))
    put("fc_w", np.concatenate([cv(f("conv_w_ffn")[l, k]) for l in range(2) for k in range(3)], axis=1))
    put("fc_b", np.concatenate([cv(f("conv_b_ffn")[l]) for l in range(2)], axis=1))
    put("eps", np.full((128, 1), EPS, np.float32))
    put("one", np.ones((128, 1), np.float32))
    put("quarter", np.full((128, 1), 0.25, np.float32))

    def lhsT(w):
        K, N = w.shape
        return np.ascontiguousarray(w.reshape(K // 128, 128, N).transpose(1, 0, 2))

    wi = f("w_in_attn")[0]
    z96 = np.zeros((1024, 64), np.float32)
    kpeA = np.concatenate([z96, wi[:, 640:672]], axis=1)
    kpeB = np.concatenate([z96, wi[:, 656:672], wi[:, 640:656]], axis=1)
    w_in = np.concatenate([wi[:, 0:640], kpeA, kpeB, wi[:, 672:2208]], axis=1)
    assert w_in.shape[1] == 2368
    wq = f("w_uq")[0].reshape(384, 8, 96)
    wqB = np.concatenate([wq[:, :, 0:64], wq[:, :, 80:96], wq[:, :, 64:80]], axis=2)
    w_uq = np.concatenate([wq, wqB], axis=2).reshape(384, 8 * 192)
    gaw = f("gate_a_w")[0]
    gxw = f("gate_x_w")[0]
    w_g = np.stack([gaw, gxw], axis=0).transpose(3, 0, 1, 2, 4)
    wup = f("w_ffn_up")
    wup_r = wup.reshape(2, 8, 128, 2, 8, 4, 128)
    wup_l = np.ascontiguousarray(wup_r.transpose(0, 4, 2, 5, 1, 3, 6)).reshape(2 * 8 * 128, 8192)
    wdn = f("w_ffn_down")
    wdn_r = wdn.reshape(2, 32, 128, 8, 128)
    wdn_l = np.ascontiguousarray(wdn_r.transpose(0, 3, 2, 1, 4)).reshape(2 * 8 * 128, 4096)
    shared = {
        "w_in": lhsT(w_in).reshape(128, -1), "w_uq": lhsT(w_uq).reshape(128, -1), "w_ukv": lhsT(f("w_ukv")[0]).reshape(128, -1),
        "w_oa": lhsT(f("w_out_attn")[0]).reshape(128, -1), "w_ir": lhsT(f("w_in_rec")[0]).reshape(128, -1),
        "w_or": lhsT(f("w_out_rec")[0]).reshape(128, -1), "w_g": np.ascontiguousarray(w_g).reshape(128, -1),
        "w_up": wup_l, "w_dn": wdn_l, "ident": np.eye(128, dtype=np.float32),
    }
    for k, shp in WSHAPES.items():
        assert list(shared[k].shape) == shp, (k, shared[k].shape, shp)
    rpb = f("na_rpb")[0]
    p = np.arange(128)
    kc = p % 64
    half = p // 64
    qc = np.arange(64)
    cs = np.clip(qc - 8, 0, 48)
    colok = (kc[:, None] >= cs[None, :]) & (kc[:, None] < cs[None, :] + 16)
    cidx = np.clip(kc[:, None] - qc[None, :] + 15, 0, 30)
    t2 = np.zeros((128, 8, 22, 64), np.float32)
    for i in range(22):
        dr = 10 - i + half
        rowok = np.abs(dr) <= 7
        ridx = np.clip(dr + 7, 0, 14)
        for h in range(8):
            v = rpb[h][ridx[:, None], cidx]
            v = np.where(rowok[:, None], v, 0.0)
            t2[:, h, i, :] = np.where(colok, v, NEG)
    shared["t2"] = t2.reshape(128, -1)

    xs = [np.ascontiguousarray(f("x_prompt")[2 * i:2 * i + 2].reshape(NT, D)) for i in range(4)] + \
         [np.ascontiguousarray(f("x_sample")[i]) for i in range(4)]
    maps = []
    t = np.arange(NT)
    for core in range(8):
        prompt = core < 4
        slen = 2048 if prompt else 4096
        seq = t // slen
        pos = (t % slen).astype(np.float32)
        inv = (10000.0 ** (-np.arange(0, 32, 2, dtype=np.float32) / 32)).astype(np.float32)
        ang = (pos[:, None] * inv[None, :]).astype(np.float32)
        c, s = np.cos(ang).T.astype(np.float32), np.sin(ang).T.astype(np.float32)
        rope = np.stack([np.concatenate([c, c], 0), np.concatenate([-s, s], 0)], 0)
        mlak = np.stack([(seq == 0), (seq == 1)], 0).astype(np.float32)
        mlaq = np.where(mlak > 0, 0.0, NEG).astype(np.float32)
        gr = t // 64
        nak = (gr[None, :] % 16 == np.arange(16)[:, None]).astype(np.float32)
        rows_seq = 32 if prompt else 64
        so = (gr // rows_seq) * rows_seq
        rs = np.clip(gr - so - 4, 0, rows_seq - 8) + so
        naq = np.full((16, NT), NEG, np.float32)
        for d in range(8):
            naq[(rs + d) % 16, t] = 0.0
        pvc = pvb.copy()
        pvc[:, PV["keep"][0]] = 0.0 if prompt else 1.0
        m = dict(shared)
        m.update({"x": xs[core], "pv": pvc, "rope": np.ascontiguousarray(rope), "mlak": mlak, "mlaq": mlaq, "nak": nak, "naq": naq})
        maps.append(m)
    return maps


def build(stop_after=None, dbg=False):
    nc = bass.Bass("TRN2", target_bir_lowering=False)
    din = lambda n, s: nc.dram_tensor(n, s, F32, kind="ExternalInput").ap()
    x_d = din("x", [NT, D])
    pv_d = din("pv", [128, NPV])
    rope_d = din("rope", [2, 32, NT])
    mlak_d, mlaq_d = din("mlak", [2, NT]), din("mlaq", [2, NT])
    nak_d, naq_d = din("nak", [16, NT]), din("naq", [16, NT])
    t2_d = din("t2", [128, 8 * 22 * 64])
    ident_d = din("ident", [128, 128])
    wf = {k: din(k, s) for k, s in WSHAPES.items()}
    y_d = nc.dram_tensor("y", [NT, D], F32, kind="ExternalOutput").ap()
    scr = lambda n, s, dt: nc.dram_tensor(n, s, dt, kind="Internal").ap()
    wb = {k: scr(k + "_b", s, BF16) for k, s in WSHAPES.items()}
    xTa, xTb = scr("xTa", [D, NT], F32), scr("xTb", [D, NT], F32)
    qna, kna = scr("qna", [512, NT], BF16), scr("kna", [512, NT], BF16)
    vna = scr("vna", [NT, 512], BF16)
    ao = scr("ao", [D, NT], BF16)
    yrec = scr("yrec", [D, NT], BF16)
    hTd = scr("hTd", [D, NT], BF16)

    sbh = nc.alloc_sbuf_tensor("arena", [128, ARENA_BYTES // 4], F32)
    AR = Arena(sbh[:], ARENA_BYTES)
    PS = [nc.alloc_psum_tensor(f"ps{i}", [128, 512], F32)[:] for i in range(8)]
    P = Prog(nc, nsems=100)

    class Rot:
        def __init__(self, banks):
            self.banks = banks
            self.i = 0

        def next(self):
            b = self.banks[self.i % len(self.banks)]
            self.i += 1
            return b

    def psk(b):
        return ("ps", b)

    WKEYS = {}

    def cast_w(name):
        rows, L = WSHAPES[name]
        step = 8192 if L % 8192 == 0 or L > 8192 else L
        for r0 in range(0, rows, 128):
            c0 = 0
            while c0 < L:
                c1 = min(L, c0 + step)
                P.dma("pool", I("dma_start", out=wb[name][r0:r0 + 128, c0:c1], in_=wf[name][r0:r0 + 128, c0:c1]),
                      writes=[("W", name, r0, c0)])
                WKEYS.setdefault((name, r0), []).append(("W", name, r0, c0))
                c0 = c1

    for nm in ["w_in", "w_uq", "w_ukv"]:
        cast_w(nm)

    pv = AR.alloc([NPV], F32)
    ident = AR.alloc([128], F32)
    ones_b = AR.alloc([128], BF16)
    P.dma("sp", I("dma_start", out=pv, in_=pv_d), writes=["pv"])
    P.dma("sp", I("dma_start", out=ident, in_=ident_d), writes=["ident"])
    P.op("pool", I("memset", ones_b, 1.0), writes=["ones"])

    def pvc(name, i=0, w=1):
        o, _ = PV[name]
        return pv[:, o + i:o + i + w]

    eps_ap = pvc("eps")
    keep_ap = pvc("keep")

    def rstd_of(srcs, n, Dn, sq, sqkey, rstd, rstdkey, rot):
        C = len(srcs)
        for c, (ap, key, isps) in enumerate(srcs):
            if isps or c % 2 == 0:
                P.op("act", I("activation", out=sq[:, c, 0:n], in_=ap, func=AF.Square),
                     reads=[] if isps else [key], writes=[(sqkey, c)] + ([key] if isps else []))
            else:
                P.op("pool", I("tensor_tensor", out=sq[:, c, 0:n], in0=ap, in1=ap, op=ALU.mult),
                     reads=[key], writes=[(sqkey, c)])
        b = rot.next()
        for c in range(C):
            P.op("pe", I("matmul", PS[b][:, 0:n], lhsT=ones_b, rhs=sq[:, c, 0:n], start=(c == 0), stop=(c == C - 1)),
                 reads=["ones", (sqkey, c)], writes=[psk(b)], inc=(c == C - 1))
        P.op("act", I("activation", out=rstd[:, 0:n], in_=PS[b][:, 0:n], func=AF.Sqrt, scale=1.0 / Dn, bias=eps_ap),
             reads=["pv"], writes=[psk(b), rstdkey])
        P.op("dve", I("reciprocal", out=rstd[:, 0:n], in_=rstd[:, 0:n]), writes=[rstdkey])

    def sq_part(srcs, n, sq, sqkey, c0=0):
        for c_, (ap, key) in enumerate(srcs):
            c = c0 + c_
            if c % 2 == 0:
                P.op("act", I("activation", out=sq[:, c, 0:n], in_=ap, func=AF.Square), reads=[key], writes=[(sqkey, c)])
            else:
                P.op("pool", I("tensor_tensor", out=sq[:, c, 0:n], in0=ap, in1=ap, op=ALU.mult), reads=[key], writes=[(sqkey, c)])

    def stat_part(C, n, Dn, sq, sqkey, rstd, rstdkey, rot, c0=0):
        b = rot.next()
        for c in range(C):
            P.op("pe", I("matmul", PS[b][:, 0:n], lhsT=ones_b, rhs=sq[:, c0 + c, 0:n], start=(c == 0), stop=(c == C - 1)),
                 reads=["ones", (sqkey, c0 + c)], writes=[psk(b)], inc=(c == C - 1))
        P.op("act", I("activation", out=rstd[:, 0:n], in_=PS[b][:, 0:n], func=AF.Sqrt, scale=1.0 / Dn, bias=eps_ap),
             reads=["pv"], writes=[psk(b), rstdkey])
        P.op("dve", I("reciprocal", out=rstd[:, 0:n], in_=rstd[:, 0:n]), writes=[rstdkey])

    def load_w(dst, name, rows0, col0, ncols, key, q="sp"):
        P.dma(q, I("dma_start", out=dst, in_=wb[name][rows0:rows0 + 128, col0:col0 + ncols]),
              reads=WKEYS[(name, rows0)], writes=[key])

    AR.push()
    qn = AR.alloc([3, NT], BF16)
    kvn = AR.alloc([2, NT], BF16)
    kpe = AR.alloc([NT], BF16)
    AR.push()
    w_in = AR.alloc([8, 2368], BF16)
    xin2 = AR.alloc([2, 4, 1024], F32)
    xT = AR.alloc([8, 512], F32)
    sq = AR.alloc([8, 512], BF16)
    hT = AR.alloc([8, 512], BF16)
    rstd = AR.alloc([512], F32)
    rstd2 = AR.alloc([512], F32)
    lat = AR.alloc([5, 512], F32)
    ropet = AR.alloc([2, 2, 512], F32)
    tmpA = AR.alloc([512], F32)
    tmpB = AR.alloc([512], F32)
    stq = AR.alloc([4, 512], BF16)
    stk = AR.alloc([4, 512], BF16)
    stv = AR.alloc([4, 512], BF16)
    rot = Rot([0, 1, 2, 3, 4, 5, 6, 7])
    gpre = lambda l, c: pvc("g_mix_pre", l * 8 + c)
    xTa_v = xTa.rearrange("(c p) t -> p c t", p=128)
    xTb_v = xTb.rearrange("(c p) t -> p c t", p=128)
    def a1_front(i):
        t0 = i * 512
        xin = xin2[:, i % 2]
        if i + 1 < 8:
            P.dma("sp", I("dma_start", out=xin2[:, (i + 1) % 2], in_=x_d[t0 + 512:t0 + 1024, :].rearrange("(s p) d -> p s d", p=128)), writes=[("xin", (i + 1) % 2)])
        P.dma("sp", I("dma_start", out=ropet[64:96, i % 2], in_=rope_d[:, :, t0:t0 + 512].rearrange("a p t -> p a t")), writes=[("ropet", i % 2)])
        for c in range(8):
            b = rot.next()
            for s in range(4):
                P.op("pe", I("transpose", out=PS[b][:, s * 128:(s + 1) * 128], in_=xin[:, s, c * 128:(c + 1) * 128], identity=ident),
                     reads=[("xin", i % 2), "ident"], writes=[psk(b)], inc=(s == 3))
            if c % 2 == 0:
                P.op("act", I("activation", out=xT[:, c, :], in_=PS[b], func=AF.Copy), writes=[psk(b), ("xT", c)])
            else:
                P.op("dve", I("tensor_copy", out=xT[:, c, :], in_=PS[b]), writes=[psk(b), ("xT", c)])
        P.dma("sp", I("dma_start", out=xTa_v[:, :, t0:t0 + 512], in_=xT), reads=[("xT", c) for c in range(8)], writes=[("xTa", i)])
        sq_part([(xT[:, c, :], ("xT", c)) for c in range(8)], 512, sq, "sq")

    def a1_stat(i):
        stat_part(8, 512, 1024, sq, "sq", rstd, "rstd", rot)

    P.dma("sp", I("dma_start", out=xin2[:, 0], in_=x_d[0:512, :].rearrange("(s p) d -> p s d", p=128)), writes=[("xin", 0)])
    load_w(w_in.rearrange("p a b -> p (a b)"), "w_in", 0, 0, 8 * 2368, "w_in")
    a1_front(0)
    a1_stat(0)
    for i in range(8):
        t0 = i * 512
        for c in range(8):
            P.op("dve", I("scalar_tensor_tensor", out=hT[:, c, :], in0=xT[:, c, :], scalar=gpre(0, c), in1=rstd, op0=ALU.mult, op1=ALU.mult),
                 reads=[("xT", c), "rstd", "pv"], writes=[("hT", c)])
        hreads = [("hT", c) for c in range(8)] + ["w_in"]

        def zmm(col0, M, b):
            for kc in range(8):
                P.op("pe", I("matmul", PS[b][0:M, :], lhsT=w_in[:, kc, col0:col0 + M], rhs=hT[:, kc, :], start=(kc == 0), stop=(kc == 7)),
                     reads=hreads, writes=[psk(b)], inc=(kc == 7))

        for j in range(5):
            b = rot.next()
            zmm(j * 128, 128, b)
            P.op("act", I("activation", out=lat[:, j, :], in_=PS[b], func=AF.Copy), writes=[psk(b), ("lat", j)])
        sq_part([(lat[:, j, :], ("lat", j)) for j in range(5)], 512, sq, "sq")
        bA = rot.next()
        zmm(640, 96, bA)
        bB = rot.next()
        zmm(736, 96, bB)
        P.op("dve", I("tensor_tensor", out=tmpA[64:96, :], in0=PS[bA][64:96, :], in1=ropet[64:96, i % 2, 0, :], op=ALU.mult),
             reads=[("ropet", i % 2)], writes=[psk(bA), "tmpA"])
        P.op("dve", I("tensor_tensor", out=tmpB[64:96, :], in0=PS[bB][64:96, :], in1=ropet[64:96, i % 2, 1, :], op=ALU.mult),
             reads=[("ropet", i % 2)], writes=[psk(bB), "tmpB"])
        P.op("pool", I("tensor_tensor", out=kpe[64:96, t0:t0 + 512], in0=tmpA[64:96, :], in1=tmpB[64:96, :], op=ALU.add),
             reads=["tmpA", "tmpB"], writes=[("kpe", i)])
        for j in range(4):
            b = rot.next()
            zmm(832 + j * 128, 128, b)
            P.op("act", I("activation", out=stq[:, j, :], in_=PS[b], func=AF.Identity, scale=0.125), writes=[psk(b), ("stq", j)])
        P.dma("sp", I("dma_start", out=qna.rearrange("(j p) t -> p j t", p=128)[:, :, t0:t0 + 512], in_=stq), reads=[("stq", j) for j in range(4)], writes=[("qna", i)])
        stat_part(3, 512, 384, sq, "sq", rstd2, "rstd2", rot)
        for j in range(3):
            P.op("dve", I("scalar_tensor_tensor", out=qn[:, j, t0:t0 + 512], in0=lat[:, j, :], scalar=pvc("q_norm", j), in1=rstd2, op0=ALU.mult, op1=ALU.mult),
                 reads=[("lat", j), "rstd2", "pv"], writes=[("qn", i)])
        stat_part(2, 512, 256, sq, "sq", rstd2, "rstd2", rot, c0=3)
        for j in range(2):
            P.op("dve", I("scalar_tensor_tensor", out=kvn[:, j, t0:t0 + 512], in0=lat[:, 3 + j, :], scalar=pvc("kv_norm", j), in1=rstd2, op0=ALU.mult, op1=ALU.mult),
                 reads=[("lat", 3 + j), "rstd2", "pv"], writes=[("kvn", i)])
        for j in range(4):
            b = rot.next()
            zmm(1344 + j * 128, 128, b)
            P.op("dve", I("tensor_copy", out=stk[:, j, :], in_=PS[b]), writes=[psk(b), ("stk", j)])
        P.dma("sp", I("dma_start", out=kna.rearrange("(j p) t -> p j t", p=128)[:, :, t0:t0 + 512], in_=stk), reads=[("stk", j) for j in range(4)], writes=[("kna", i)])
        if i + 1 < 8:
            a1_front(i + 1)
        for s in range(4):
            b = rot.next()
            for kc in range(8):
                P.op("pe", I("matmul", PS[b], lhsT=hT[:, kc, s * 128:(s + 1) * 128], rhs=w_in[:, kc, 1856:2368], start=(kc == 0), stop=(kc == 7)),
                     reads=hreads, writes=[psk(b)], inc=(kc == 7))
            if s % 2 == 0:
                P.op("act", I("activation", out=stv[:, s, :], in_=PS[b], func=AF.Copy), writes=[psk(b), ("stv", s)])
            else:
                P.op("dve", I("tensor_copy", out=stv[:, s, :], in_=PS[b]), writes=[psk(b), ("stv", s)])
        P.dma("sp", I("dma_start", out=vna[t0:t0 + 512, :].rearrange("(s p) f -> p s f", p=128), in_=stv), reads=[("stv", s) for s in range(4)], writes=[("vna", i)])
        if i + 1 < 8:
            a1_stat(i + 1)
    P.barrier()
    AR.pop()
    if stop_after == "A1":
        return finish(nc, P, AR, dbg, locals())

    AR.push()
    w_uq = AR.alloc([3, 8, 192], BF16)
    w_ukv = AR.alloc([2, 1024], BF16)
    load_w(w_uq.rearrange("p a b c -> p (a b c)"), "w_uq", 0, 0, 3 * 8 * 192, "w_uq")
    load_w(w_ukv.rearrange("p a b -> p (a b)"), "w_ukv", 0, 0, 2048, "w_ukv")
    KT = AR.alloc([2, NT], BF16)
    VA = AR.alloc([2, 32, 128], BF16)
    QT = AR.alloc([2, NT], BF16)
    PT = AR.alloc([6, 512], BF16)
    ropef = AR.alloc([2, NT], F32)
    Rr = AR.alloc([2, 512], F32)
    ost = AR.alloc([8, 512], BF16)
    tA = AR.alloc([2, 512], F32)
    tB = AR.alloc([2, 512], F32)
    P.dma("sp", I("dma_start", out=ropef[64:96], in_=rope_d.rearrange("a p t -> p a t")), writes=["ropef"])
    for b2 in range(2):
        P.op("act", I("activation", out=KT[64:96, b2, :], in_=kpe[64:96, :], func=AF.Copy), writes=[("KTpe", b2)])
        P.dma("pool", I("dma_start", out=KT[96:98, b2, :], in_=mlak_d), writes=[("KTaug", b2)])
        P.dma("pool", I("dma_start", out=QT[96:98, b2, :], in_=mlaq_d), writes=[("QTaug", b2)])
        P.op("pool", I("memset", VA[:, b2, :, 64:128], 1.0), writes=[("VAone", b2)])
    for nm in ["w_oa", "w_up", "w_dn", "w_ir", "w_g", "w_or"]:
        cast_w(nm)
    rotS = Rot([0, 1, 2, 3, 4, 5])
    obank = Rot([6, 7])
    scale = 96.0 ** -0.5
    pt_i = 0

    def mla_prep(h):
        hb = h % 2
        for i in range(8 if A2_S & 1 else 0):
            b = rotS.next()
            for kc in range(2):
                P.op("pe", I("matmul", PS[b][0:64, :], lhsT=w_ukv[:, kc, h * 128:h * 128 + 64], rhs=kvn[:, kc, i * 512:(i + 1) * 512], start=(kc == 0), stop=(kc == 1)),
                     reads=["w_ukv"], writes=[psk(b)], inc=(kc == 1))
            P.op("dve", I("tensor_copy", out=KT[0:64, hb, i * 512:(i + 1) * 512], in_=PS[b][0:64, :]), writes=[psk(b), ("KT", hb)])
        for g in range(4 if A2_S & 2 else 0):
            b = rotS.next()
            for s in range(8):
                tc = g * 8 + s
                for kc in range(2):
                    P.op("pe", I("matmul", PS[b][:, s * 64:(s + 1) * 64], lhsT=kvn[:, kc, tc * 128:(tc + 1) * 128], rhs=w_ukv[:, kc, h * 128 + 64:h * 128 + 128], start=(kc == 0), stop=(kc == 1)),
                         reads=["w_ukv"], writes=[psk(b)], inc=(kc == 1 and s == 7))
            P.op("dve", I("tensor_copy", out=VA[:, hb, g * 8:(g + 1) * 8, 0:64], in_=PS[b].rearrange("p (s f) -> p s f", f=64)),
                 writes=[psk(b), ("VA", hb)])
        for i in range(8 if A2_S & 4 else 0):
            sl = slice(i * 512, (i + 1) * 512)
            bA = rotS.next()
            for kc in range(3):
                P.op("pe", I("matmul", PS[bA][0:96, :], lhsT=w_uq[:, kc, h, 0:96], rhs=qn[:, kc, sl], start=(kc == 0), stop=(kc == 2)),
                     reads=["w_uq"], writes=[psk(bA)], inc=(kc == 2))
            bB = rotS.next()
            for kc in range(3):
                P.op("pe", I("matmul", PS[bB][0:96, :], lhsT=w_uq[:, kc, h, 96:192], rhs=qn[:, kc, sl], start=(kc == 0), stop=(kc == 2)),
                     reads=["w_uq"], writes=[psk(bB)], inc=(kc == 2))
            P.op("dve", I("tensor_copy", out=QT[0:64, hb, sl], in_=PS[bA][0:64, :]), writes=[psk(bA), ("QT", hb)])
            P.op("dve", I("tensor_tensor", out=tA[64:96, i % 2, :], in0=PS[bA][64:96, :], in1=ropef[64:96, 0, sl], op=ALU.mult),
                 reads=["ropef"], writes=[psk(bA), ("tA", i % 2)])
            P.op("dve", I("tensor_tensor", out=tB[64:96, i % 2, :], in0=PS[bB][64:96, :], in1=ropef[64:96, 1, sl], op=ALU.mult),
                 reads=["ropef"], writes=[psk(bB), ("tB", i % 2)])
            P.op("dve", I("tensor_tensor", out=QT[64:96, hb, sl], in0=tA[64:96, i % 2, :], in1=tB[64:96, i % 2, :], op=ALU.add),
                 reads=[("tA", i % 2), ("tB", i % 2)], writes=[("QT", hb)])

    mla_prep(0)
    for h in range(A2_HEADS):
        hb = h % 2
        kread = [("KT", hb), ("KTpe", hb), ("KTaug", hb), ("QT", hb), ("QTaug", hb)]
        for qb in range(A2_QB):
            if qb == 3 and h + 1 < A2_HEADS:
                mla_prep(h + 1)
            qs = slice(qb * 512, (qb + 1) * 512)
            ob = obank.next()
            sb_of = {}

            def qk(kc):
                b = rotS.next()
                sb_of[kc] = b
                P.op("pe", I("matmul", PS[b], lhsT=KT[0:98, hb, kc * 128:(kc + 1) * 128], rhs=QT[0:98, hb, qs], start=True, stop=True),
                     reads=kread, writes=[psk(b)])

            qk(0)
            qk(1)
            for kc in range(32):
                if kc + 2 < 32:
                    qk(kc + 2)
                b = sb_of[kc]
                slot = pt_i % 6
                pt_i += 1
                P.op("act", I("activation", out=PT[:, slot, :], in_=PS[b], func=AF.Exp, scale=scale), writes=[psk(b), ("PT", slot)])
                P.op("pe", I("matmul", PS[ob], lhsT=VA[:, hb, kc, :], rhs=PT[:, slot, :], start=(kc == 0), stop=(kc == 31)),
                     reads=[("PT", slot), ("VA", hb), ("VAone", hb)], writes=[psk(ob)], inc=(kc == 31))
            r = qb % 2
            P.op("dve", I("reciprocal", out=Rr[64:128, r, :], in_=PS[ob][64:128, :]), writes=[psk(ob), ("Rr", r)])
            P.op("dve", I("tensor_tensor", out=ost[0:64, qb, :], in0=PS[ob][0:64, :], in1=Rr[64:128, r, :], op=ALU.mult),
                 reads=[("Rr", r)], writes=[psk(ob), ("ost", qb)])
            P.dma("sp", I("dma_start", out=ao[h * 64:(h + 1) * 64, qs], in_=ost[0:64, qb, :]), reads=[("ost", qb)], writes=[("ao", h, qb)])
    P.barrier()
    AR.pop()
    AR.pop()
    if stop_after == "A2":
        return finish(nc, P, AR, dbg, locals())

    AR.push()
    T2 = AR.alloc([8, 22, 64], BF16)
    identb = AR.alloc([128], BF16)
    KN = AR.alloc([2, NT], BF16)
    QN = AR.alloc([2, NT], BF16)
    VN = AR.alloc([2, 32, 128], BF16)
    PT = AR.alloc([4, 512], BF16)
    Rr = AR.alloc([2, 512], F32)
    ost = AR.alloc([8, 512], BF16)
    P.dma("pool", I("dma_start", out=T2.rearrange("p a b c -> p (a b c)"), in_=t2_d), writes=["T2"])
    P.op("dve", I("tensor_copy", out=identb, in_=ident), reads=["ident"], writes=["identb"])
    for b2 in range(2):
        P.dma("pool", I("dma_start", out=KN[64:80, b2, :], in_=nak_d), writes=[("KNaug", b2)])
        P.dma("pool", I("dma_start", out=QN[64:80, b2, :], in_=naq_d), writes=[("QNaug", b2)])
        P.op("pool", I("memset", VN[:, b2, :, 64:128], 1.0), writes=[("VNone", b2)])
    rotS = Rot([0, 1, 2, 3, 4, 5])
    obank = Rot([6, 7])
    vna_v = vna.rearrange("(c p) f -> p c f", p=128)
    st_i = 0
    def na_load(h):
        hb = h % 2
        P.dma("sp", I("dma_start", out=KN[0:64, hb, :], in_=kna[h * 64:(h + 1) * 64, :]), writes=[("KN", hb)])
        P.dma("sp", I("dma_start", out=QN[0:64, hb, :], in_=qna[h * 64:(h + 1) * 64, :]), writes=[("QN", hb)])
        for g in range(4):
            P.dma("sp", I("dma_start", out=VN[:, hb, g * 8:(g + 1) * 8, 0:64], in_=vna_v[:, g * 8:(g + 1) * 8, h * 64:(h + 1) * 64]), writes=[("VN", hb, g)])

    for h in range(8):
        hb = h % 2
        if h == 0:
            na_load(0)
        if h + 1 < 8:
            na_load(h + 1)
        kread = [("KN", hb), ("KNaug", hb), ("QN", hb), ("QNaug", hb)]
        for qb in range(8):
            qs = slice(qb * 512, (qb + 1) * 512)
            ob = obank.next()
            cl = list(range(max(0, 4 * qb - 2), min(32, 4 * qb + 6)))
            sb_of = {}

            def qk(ci):
                c = cl[ci]
                b = rotS.next()
                sb_of[ci] = b
                i0 = 10 - (2 * c - 8 * qb)
                P.op("pe", I("matmul", PS[b], lhsT=KN[0:80, hb, c * 128:(c + 1) * 128], rhs=QN[0:80, hb, qs], start=True, stop=False),
                     reads=kread, writes=[psk(b)], inc=False)
                P.op("pe", I("matmul", PS[b], lhsT=identb, rhs=T2[:, h, i0:i0 + 8, :].rearrange("p a b -> p (a b)"), start=False, stop=True),
                     reads=["identb", "T2"], writes=[psk(b)])

            qk(0)
            qk(1)
            for ci, c in enumerate(cl):
                if ci + 2 < len(cl):
                    qk(ci + 2)
                b = sb_of[ci]
                slot = st_i % 4
                st_i += 1
                P.op("act", I("activation", out=PT[:, slot, :], in_=PS[b], func=AF.Exp), writes=[psk(b), ("PT", slot)])
                P.op("pe", I("matmul", PS[ob], lhsT=VN[:, hb, c, :], rhs=PT[:, slot, :], start=(ci == 0), stop=(ci == len(cl) - 1)),
                     reads=[("PT", slot), ("VN", hb, c // 8), ("VNone", hb)], writes=[psk(ob)], inc=(ci == len(cl) - 1))
            r = qb % 2
            P.op("dve", I("reciprocal", out=Rr[64:128, r, :], in_=PS[ob][64:128, :]), writes=[psk(ob), ("Rr", r)])
            P.op("dve", I("tensor_tensor", out=ost[0:64, qb, :], in0=PS[ob][0:64, :], in1=Rr[64:128, r, :], op=ALU.mult),
                 reads=[("Rr", r)], writes=[psk(ob), ("ost", qb)])
            P.dma("sp", I("dma_start", out=ao[512 + h * 64:512 + (h + 1) * 64, qs], in_=ost[0:64, qb, :]), reads=[("ost", qb)], writes=[("ao", 8 + h, qb)])
    P.barrier()
    AR.pop()
    if stop_after == "A3":
        return finish(nc, P, AR, dbg, locals())

    def fused_tiles(l, src_d, wo_name, xin_d, xout_d, final):
        AR.push()
        w_o = AR.alloc([8, 1024], BF16)
        load_w(w_o.rearrange("p a b -> p (a b)"), wo_name, 0, 0, 8192, "w_o")
        NM = 412
        srcT = AR.alloc([8, NM], BF16)
        xTt = AR.alloc([8, NM], F32)
        fT = AR.alloc([8, 410], F32)
        x1d = AR.alloc([2, 8, NM], F32)
        sqt = AR.alloc([8, NM], BF16)
        sqe = AR.alloc([8, 410], BF16)
        rs1 = AR.alloc([NM], F32)
        rs2 = AR.alloc([NM], F32)
        rs3 = AR.alloc([NM], F32)
        h2 = AR.alloc([8, NM], BF16)
        aT = AR.alloc([32, 410], BF16)
        wup = AR.alloc([2, 4, 8, 256], BF16)
        wdn = AR.alloc([2, 32, 128], BF16)
        t1 = AR.alloc([4, 410], F32)
        ug = AR.alloc([2, 410], F32)
        uv = AR.alloc([2, 410], F32)
        gg = AR.alloc([2, 410], F32)
        yst = AR.alloc([2, 1024], F32) if final else None
        src_v = src_d.rearrange("(c p) t -> p c t", p=128)
        xin_v = xin_d.rearrange("(c p) t -> p c t", p=128)
        xout_v = None if final else xout_d.rearrange("(c p) t -> p c t", p=128)
        rot = Rot([0, 1, 2, 3, 4, 5, 6, 7])
        gpost = lambda c: pvc("g_mix_post", l * 8 + c)
        gfpre = lambda c: pvc("g_ffn_pre", l * 8 + c)
        gfpost = lambda c: pvc("g_ffn_post", l * 8 + c)
        fcw = lambda k, ch: pvc("fc_w", (l * 3 + k) * 64 + ch)
        fcb = lambda ch: pvc("fc_b", l * 64 + ch)
        cnt = {"wup": 0, "wdn": 0}

        def issue_wup(jg, slot):
            P.dma("sp", I("dma_start", out=wup[:, slot].rearrange("p a b c -> p (a b c)"), in_=wb["w_up"][(l * 8 + jg) * 128:(l * 8 + jg + 1) * 128, :]),
                  reads=WKEYS[("w_up", (l * 8 + jg) * 128)], writes=[("wup", slot)])

        def issue_wdn(oc, slot):
            P.dma("sp", I("dma_start", out=wdn[:, slot].rearrange("p a b -> p (a b)"), in_=wb["w_dn"][(l * 8 + oc) * 128:(l * 8 + oc + 1) * 128, :]),
                  reads=WKEYS[("w_dn", (l * 8 + oc) * 128)], writes=[("wdn", slot)])

        tiles = []
        t0 = 0
        for ti, n in enumerate(FT_TILES):
            tiles.append((ti, t0, n))
            t0 += n
        h2all = [("h2", c) for c in range(8)]
        aall = [("aT", j) for j in range(32)]

        def P_s0(ti, t0, n):
            n2 = n + 2
            x1 = x1d[:, ti % 2]
            xk = ("x1", ti % 2)
            lo = max(t0 - 1, 0)
            hi = min(t0 + n + 1, NT)
            j0 = lo - (t0 - 1)
            j1 = j0 + (hi - lo)
            P.dma("sp", I("dma_start", out=srcT[:, :, j0:j1], in_=src_v[:, :, lo:hi]), writes=["srcT"])
            P.dma("sp", I("dma_start", out=xTt[:, :, j0:j1], in_=xin_v[:, :, lo:hi]), writes=["xTt"])
            if j0 > 0:
                P.op("pool", I("memset", srcT[:, :, 0:1], 0.0), writes=["srcT"])
                P.op("pool", I("memset", xTt[:, :, 0:1], 0.0), writes=["xTt"])
            if j1 < n2:
                P.op("pool", I("memset", srcT[:, :, j1:n2], 0.0), writes=["srcT"])
                P.op("pool", I("memset", xTt[:, :, j1:n2], 0.0), writes=["xTt"])
            issue_wup(0, cnt["wup"] % 2)
            for oc in range(8):
                b = rot.next()
                for kc in range(8):
                    P.op("pe", I("matmul", PS[b][:, 0:n2], lhsT=w_o[:, kc, oc * 128:(oc + 1) * 128], rhs=srcT[:, kc, 0:n2], start=(kc == 0), stop=(kc == 7)),
                         reads=["w_o", "srcT"], writes=[psk(b)], inc=(kc == 7))
                P.op("act", I("activation", out=x1[:, oc, 0:n2], in_=PS[b][:, 0:n2], func=AF.Copy), writes=[psk(b), (xk, oc)])
            sq_part([(x1[:, c, 0:n2], (xk, c)) for c in range(8)], n2, sqt, "sqt")

        def P_s1(ti, t0, n):
            n2 = n + 2
            x1 = x1d[:, ti % 2]
            xk = ("x1", ti % 2)
            stat_part(8, n2, 1024, sqt, "sqt", rs1, "rs1", rot)
            for c in range(8):
                P.op("dve", I("scalar_tensor_tensor", out=x1[:, c, 0:n2], in0=x1[:, c, 0:n2], scalar=gpost(c), in1=rs1[:, 0:n2], op0=ALU.mult, op1=ALU.mult),
                     reads=["rs1", "pv"], writes=[(xk, c)])
                P.op("pool", I("tensor_tensor", out=x1[:, c, 0:n2], in0=x1[:, c, 0:n2], in1=xTt[:, c, 0:n2], op=ALU.add),
                     reads=["xTt"], writes=[(xk, c)])
            sq_part([(x1[:, c, 0:n2], (xk, c)) for c in range(8)], n2, sqt, "sqt")

        def P_s2(ti, t0, n):
            n2 = n + 2
            x1 = x1d[:, ti % 2]
            xk = ("x1", ti % 2)
            stat_part(8, n2, 1024, sqt, "sqt", rs2, "rs2", rot)
            for c in range(8):
                P.op("dve", I("scalar_tensor_tensor", out=h2[:, c, 0:n2], in0=x1[:, c, 0:n2], scalar=gfpre(c), in1=rs2[:, 0:n2], op0=ALU.mult, op1=ALU.mult),
                     reads=[(xk, c), "rs2", "pv"], writes=[("h2", c)])
            if t0 == 0:
                P.op("pool", I("memset", h2[:, :, 0:1], 0.0), writes=h2all)
            elif t0 == 2048:
                P.op("pool", I("tensor_scalar", out=h2[:, :, 0:1], in0=h2[:, :, 0:1], scalar1=keep_ap, scalar2=None, op0=ALU.mult), reads=["pv"], writes=h2all)
            if t0 + n == NT:
                P.op("pool", I("memset", h2[:, :, n + 1:n + 2], 0.0), writes=h2all)
            elif t0 + n == 2048:
                P.op("pool", I("tensor_scalar", out=h2[:, :, n + 1:n + 2], in0=h2[:, :, n + 1:n + 2], scalar1=keep_ap, scalar2=None, op0=ALU.mult), reads=["pv"], writes=h2all)

        def U_group(ti, t0, n, jg):
            n2 = n + 2
            slot = cnt["wup"] % 2
            cnt["wup"] += 1
            if jg + 1 < 8:
                issue_wup(jg + 1, cnt["wup"] % 2)
            else:
                issue_wdn(0, cnt["wdn"] % 2)
            for jj in range(4):
                j = jg * 4 + jj
                bg = rot.next()
                for kc in range(8):
                    P.op("pe", I("matmul", PS[bg][:, 0:n2], lhsT=wup[:, slot, jj, kc, 0:128], rhs=h2[:, kc, 0:n2], start=(kc == 0), stop=(kc == 7)),
                         reads=h2all + [("wup", slot)], writes=[psk(bg)], inc=(kc == 7))
                bv = rot.next()
                for kc in range(8):
                    P.op("pe", I("matmul", PS[bv][:, 0:n2], lhsT=wup[:, slot, jj, kc, 128:256], rhs=h2[:, kc, 0:n2], start=(kc == 0), stop=(kc == 7)),
                         reads=h2all + [("wup", slot)], writes=[psk(bv)], inc=(kc == 7))
                q = j % 2
                for (bb, ch, tq, uo, ukey) in ((bg, j, 2 * q, ug, "ug"), (bv, 32 + j, 2 * q + 1, uv, "uv")):
                    P.op("act", I("activation", out=t1[:, tq, 0:n], in_=PS[bb][:, 1:n + 1], func=AF.Identity, scale=fcw(1, ch), bias=fcb(ch)),
                         reads=["pv"], writes=[psk(bb), ("t1", tq)])
                    P.op("dve", I("scalar_tensor_tensor", out=t1[:, tq, 0:n], in0=PS[bb][:, 0:n], scalar=fcw(0, ch), in1=t1[:, tq, 0:n], op0=ALU.mult, op1=ALU.add),
                         reads=["pv"], writes=[psk(bb), ("t1", tq)])
                    P.op("dve", I("scalar_tensor_tensor", out=uo[:, q, 0:n], in0=PS[bb][:, 2:n + 2], scalar=fcw(2, ch), in1=t1[:, tq, 0:n], op0=ALU.mult, op1=ALU.add),
                         reads=["pv", ("t1", tq)], writes=[psk(bb), (ukey, q)])
                P.op("act", I("activation", out=gg[:, q, 0:n], in_=ug[:, q, 0:n], func=AF.Gelu_apprx_tanh), reads=[("ug", q)], writes=[("gg", q)])
                P.op("pool", I("tensor_tensor", out=aT[:, j, 0:n], in0=gg[:, q, 0:n], in1=uv[:, q, 0:n], op=ALU.mult),
                     reads=[("gg", q), ("uv", q)], writes=[("aT", j)])

        def D_oc(ti, t0, n, oc):
            slot = cnt["wdn"] % 2
            cnt["wdn"] += 1
            if oc + 1 < 8:
                issue_wdn(oc + 1, cnt["wdn"] % 2)
            b = rot.next()
            for fc in range(32):
                P.op("pe", I("matmul", PS[b][:, 0:n], lhsT=wdn[:, slot, fc, :], rhs=aT[:, fc, 0:n], start=(fc == 0), stop=(fc == 31)),
                     reads=aall + [("wdn", slot)], writes=[psk(b)], inc=(fc == 31))
            P.op("act", I("activation", out=fT[:, oc, 0:n], in_=PS[b][:, 0:n], func=AF.Copy), writes=[psk(b), ("fT", oc)])

        def E_e0(ti, t0, n):
            sq_part([(fT[:, c, 0:n], ("fT", c)) for c in range(8)], n, sqe, "sqe")

        def E_e1(ti, t0, n):
            x1 = x1d[:, ti % 2]
            xk = ("x1", ti % 2)
            stat_part(8, n, 1024, sqe, "sqe", rs3, "rs3", rot)
            for c in range(8):
                P.op("dve", I("scalar_tensor_tensor", out=fT[:, c, 0:n], in0=fT[:, c, 0:n], scalar=gfpost(c), in1=rs3[:, 0:n], op0=ALU.mult, op1=ALU.mult),
                     reads=["rs3", "pv"], writes=[("fT", c)])
                P.op("pool", I("tensor_tensor", out=x1[:, c, 1:n + 1], in0=x1[:, c, 1:n + 1], in1=fT[:, c, 0:n], op=ALU.add),
                     reads=[("fT", c)], writes=[(xk, c)])
            if not final:
                P.dma("sp", I("dma_start", out=xout_v[:, :, t0:t0 + n], in_=x1[:, :, 1:n + 1]), reads=[(xk, c) for c in range(8)], writes=[("xout", ti)])
                sq_part([(x1[:, c, 1:n + 1], (xk, c)) for c in range(8)], n, sqe, "sqe")

        def E_e2(ti, t0, n):
            if not final:
                x1 = x1d[:, ti % 2]
                xk = ("x1", ti % 2)
                hst = fT.bitcast(BF16)
                stat_part(8, n, 1024, sqe, "sqe", rs3, "rs3", rot)
                for c in range(8):
                    P.op("dve", I("scalar_tensor_tensor", out=hst[:, c, 0:n], in0=x1[:, c, 1:n + 1], scalar=pvc("g_mix_pre", (l + 1) * 8 + c), in1=rs3[:, 0:n], op0=ALU.mult, op1=ALU.mult),
                         reads=[(xk, c), "rs3", "pv"], writes=[("fT", c)])
                P.dma("sp", I("dma_start", out=hTd.rearrange("(c p) t -> p c t", p=128)[:, :, t0:t0 + n], in_=hst[:, :, 0:n]), reads=[("fT", c) for c in range(8)], writes=[("hTd", ti)])
                return
            x1 = x1d[:, ti % 2]
            xk = ("x1", ti % 2)
            s0 = 0
            si = 0
            while s0 < n:
                w = min(128, n - s0)
                ys = si % 2
                for half in range(2):
                    b = rot.next()
                    for c4 in range(4):
                        c = half * 4 + c4
                        P.op("pe", I("transpose", out=PS[b][0:w, c4 * 128:(c4 + 1) * 128], in_=x1[:, c, 1 + s0:1 + s0 + w], identity=ident),
                             reads=[(xk, c), "ident"], writes=[psk(b)], inc=(c4 == 3))
                    if half == 0:
                        P.op("act", I("activation", out=yst[0:w, ys, 0:512], in_=PS[b][0:w, :], func=AF.Copy), writes=[psk(b), ("yst", ys, 0)])
                    else:
                        P.op("dve", I("tensor_copy", out=yst[0:w, ys, 512:1024], in_=PS[b][0:w, :]), writes=[psk(b), ("yst", ys, 1)])
                P.dma("sp", I("dma_start", out=y_d[t0 + s0:t0 + s0 + w, :], in_=yst[0:w, ys, :]), reads=[("yst", ys, 0), ("yst", ys, 1)], writes=[("y", ti, si)])
                s0 += w
                si += 1

        P_s0(*tiles[0])
        P_s1(*tiles[0])
        P_s2(*tiles[0])
        for k, tl in enumerate(tiles):
            prev = tiles[k - 1] if k > 0 else None
            nxt = tiles[k + 1] if k + 1 < len(tiles) else None
            for jg in range(8):
                U_group(*tl, jg)
                if prev is not None and jg == 0:
                    E_e1(*prev)
                if prev is not None and jg == 2:
                    E_e2(*prev)
            if nxt is not None:
                P_s0(*nxt)
            D_oc(*tl, 0)
            D_oc(*tl, 1)
            if nxt is not None:
                P_s1(*nxt)
            D_oc(*tl, 2)
            D_oc(*tl, 3)
            if nxt is not None:
                P_s2(*nxt)
            for oc in range(4, 8):
                D_oc(*tl, oc)
            E_e0(*tl)
        E_e1(*tiles[-1])
        E_e2(*tiles[-1])
        P.barrier()
        AR.pop()

    fused_tiles(0, ao, "w_oa", xTa, xTb, False)
    if stop_after == "F0":
        return finish(nc, P, AR, dbg, locals())

    AR.push()
    hTa = AR.alloc([8, NT], BF16)
    AR.push()
    rot = Rot([0, 1, 2, 3, 4, 5, 6, 7])
    for c in range(8):
        P.dma("sp", I("dma_start", out=hTa[:, c, :], in_=hTd[c * 128:(c + 1) * 128, :]), writes=[("hTa", c)])
    P.barrier()
    AR.pop()
    wg = AR.alloc([2, 2, 8, 128], BF16)
    load_w(wg.rearrange("p a b c d -> p (a b c d)"), "w_g", 0, 0, 4096, "wg")
    wir = AR.alloc([2, 8, 2, 128], BF16)
    pA = AR.alloc([4104], F32)
    xbc = AR.alloc([NT], F32)
    xbb = AR.alloc([NT], BF16)
    Ib = AR.alloc([NT], F32)
    Bt = AR.alloc([NT], F32)
    Hf = AR.alloc([NT], F32)
    Gb = AR.alloc([NT], BF16)
    cl = AR.alloc([16], F32)
    P.op("act", I("activation", out=cl, in_=pvc("lam", 0, 16), func=AF.Exp, scale=-1.0), reads=["pv"], writes=["cl"])
    P.op("act", I("activation", out=cl, in_=cl, func=AF.Ln, bias=pvc("one")), reads=["pv"], writes=["cl"])
    P.op("dve", I("tensor_scalar", out=cl, in0=cl, scalar1=-4.0, scalar2=None, op0=ALU.mult), writes=["cl"])
    hga = AR.alloc([16], F32)
    hgx = AR.alloc([16], F32)
    P.op("dve", I("tensor_scalar", out=hga, in0=pvc("ga_b", 0, 16), scalar1=0.5, scalar2=None, op0=ALU.mult), reads=["pv"], writes=["hga"])
    P.op("dve", I("tensor_scalar", out=hgx, in0=pvc("gx_b", 0, 16), scalar1=0.5, scalar2=None, op0=ALU.mult), reads=["pv"], writes=["hgx"])
    quarter_ap = pvc("quarter")
    wir_v = wb["w_ir"].rearrange("p (k t n) -> p k t n", k=8, t=2)
    A = pA
    one_ap = pvc("one")
    SEG = 1024
    for c in range(8):
        ws = c % 2
        if c == 0:
            P.dma("sp", I("dma_start", out=wir[:, 0], in_=wir_v[:, :, :, 0:128]), reads=WKEYS[("w_ir", 0)], writes=[("wir", 0)])
        if c + 1 < 8:
            P.dma("sp", I("dma_start", out=wir[:, (c + 1) % 2], in_=wir_v[:, :, :, (c + 1) * 128:(c + 2) * 128]), reads=WKEYS[("w_ir", 0)], writes=[("wir", (c + 1) % 2)])
        Akeys = [("A", g) for g in range(4)]
        for i in range(8):
            sl = slice(i * 512, (i + 1) * 512)
            dst0 = 1 + i * 512 if i < 4 else 2052 + (i - 4) * 512
            b = rot.next()
            for kc in range(8):
                P.op("pe", I("matmul", PS[b], lhsT=wir[:, ws, kc, 1, :], rhs=hTa[:, kc, sl], start=(kc == 0), stop=(kc == 7)),
                     reads=[("wir", ws)], writes=[psk(b)], inc=(kc == 7))
            P.op("dve", I("tensor_copy", out=pA[:, dst0:dst0 + 512], in_=PS[b]), writes=[psk(b), "pA"] + Akeys)
        P.op("pool", I("memset", pA[:, 0:1], 0.0), writes=["pA"])
        P.op("pool", I("memset", pA[:, 4100:4102], 0.0), writes=["pA"])
        P.op("dve", I("tensor_scalar", out=pA[:, 2049:2051], in0=pA[:, 2052:2054], scalar1=keep_ap, scalar2=None, op0=ALU.mult), reads=["pv"], writes=["pA"])
        P.op("dve", I("tensor_scalar", out=pA[:, 2051:2052], in0=pA[:, 2048:2049], scalar1=keep_ap, scalar2=None, op0=ALU.mult), reads=["pv"], writes=["pA"])
        rcw = lambda k: pvc("rc_w", k * 8 + c)
        for sg in range(4):
            base = (0 if sg < 2 else 2051) + (sg % 2) * SEG
            o = slice(sg * SEG, (sg + 1) * SEG)
            P.op("act", I("activation", out=xbc[:, o], in_=pA[:, base + 1:base + 1 + SEG], func=AF.Identity, scale=rcw(1), bias=pvc("rc_b", c)),
                 reads=["pA", "pv"], writes=[("xbc", sg)])
        for sg in range(4):
            base = (0 if sg < 2 else 2051) + (sg % 2) * SEG
            o = slice(sg * SEG, (sg + 1) * SEG)
            for k in (0, 2, 3):
                P.op("dve", I("scalar_tensor_tensor", out=xbc[:, o], in0=pA[:, base + k:base + k + SEG], scalar=rcw(k), in1=xbc[:, o], op0=ALU.mult, op1=ALU.add),
                     reads=["pA", "pv"], writes=[("xbc", sg)] + ([("pAfree", sg)] if k == 3 else []))
            P.op("act", I("activation", out=xbb[:, o], in_=xbc[:, o], func=AF.Copy), reads=[("xbc", sg)], writes=[("xbb", sg)])
        pafree = [("pAfree", g) for g in range(4)]
        for e_ in range(2):
            segs = (0, 1, 2, 3) if e_ == 0 else (3, 2, 1, 0)
            hcl_e = cl[:, e_ * 8 + c:e_ * 8 + c + 1]
            for sg in segs:
                for i in (2 * sg, 2 * sg + 1):
                    sl = slice(i * 512, (i + 1) * 512)
                    b = rot.next()
                    P.op("pe", I("matmul", PS[b], lhsT=wg[:, 0, e_, c, :], rhs=xbb[:, sl], start=True, stop=True), reads=["wg", ("xbb", sg)], writes=[psk(b)])
                    P.op("act", I("activation", out=A[:, sl], in_=PS[b], func=AF.Tanh, scale=0.5, bias=hga[:, e_ * 8 + c:e_ * 8 + c + 1]), reads=["hga"] + pafree, writes=[psk(b), ("A", sg)])
                    b = rot.next()
                    P.op("pe", I("matmul", PS[b], lhsT=wg[:, 1, e_, c, :], rhs=xbb[:, sl], start=True, stop=True), reads=["wg", ("xbb", sg)], writes=[psk(b)])
                    P.op("act", I("activation", out=Ib[:, sl], in_=PS[b], func=AF.Tanh, scale=0.5, bias=hgx[:, e_ * 8 + c:e_ * 8 + c + 1]), reads=["hgx"], writes=[psk(b), ("Ib", sg)])
            for sg in segs:
                o = slice(sg * SEG, (sg + 1) * SEG)
                P.op("act", I("activation", out=A[:, o], in_=A[:, o], func=AF.Exp, scale=hcl_e, bias=hcl_e), reads=["cl"], writes=[("A", sg)])
                P.op("dve", I("scalar_tensor_tensor", out=Bt[:, o], in0=A[:, o], scalar=-1.0, in1=A[:, o], op0=ALU.mult, op1=ALU.mult), reads=[("A", sg)], writes=[("Bt", sg)])
            for sg in segs:
                o = slice(sg * SEG, (sg + 1) * SEG)
                P.op("act", I("activation", out=Bt[:, o], in_=Bt[:, o], func=AF.Sqrt, scale=0.25, bias=quarter_ap), reads=["pv"], writes=[("Bt", sg)])
                P.op("dve", I("scalar_tensor_tensor", out=Bt[:, o], in0=Ib[:, o], scalar=1.0, in1=Bt[:, o], op0=ALU.add, op1=ALU.mult), reads=[("Ib", sg)], writes=[("Bt", sg)])
                P.op("dve", I("scalar_tensor_tensor", out=Bt[:, o], in0=Bt[:, o], scalar=1.0, in1=xbc[:, o], op0=ALU.mult, op1=ALU.mult), reads=[("xbc", sg)], writes=[("Bt", sg)])
                if e_ == 0:
                    if sg == 2:
                        P.op("dve", I("tensor_scalar", out=A[:, 2048:2049], in0=A[:, 2048:2049], scalar1=keep_ap, scalar2=None, op0=ALU.mult), reads=["pv"], writes=[("A", sg)])
                    init = 0.0 if sg == 0 else Hf[:, sg * SEG - 1:sg * SEG]
                    P.op("dve", I("tensor_tensor_scan", out=Hf[:, o], data0=A[:, o], data1=Bt[:, o], initial=init, op0=ALU.mult, op1=ALU.add),
                         reads=[("A", sg), ("Bt", sg)] + ([("Hf", sg - 1)] if sg > 0 else []), writes=[("Hf", sg)])
                else:
                    if sg == 1:
                        P.op("dve", I("tensor_scalar", out=A[:, 2047:2048], in0=A[:, 2047:2048], scalar1=keep_ap, scalar2=None, op0=ALU.mult), reads=["pv"], writes=[("A", sg)])
                    lo, hi = sg * SEG, (sg + 1) * SEG
                    init = 0.0 if sg == 3 else Ib[:, hi:hi + 1]
                    rv = lambda ap, lo=lo, hi=hi: ap[:, lo:hi][:, ::-1]
                    P.op("dve", I("tensor_tensor_scan", out=rv(Ib), data0=rv(A), data1=rv(Bt), initial=init, op0=ALU.mult, op1=ALU.add),
                         reads=[("A", sg), ("Bt", sg)] + ([("Ib", sg + 1)] if sg < 3 else []), writes=[("Ib", sg)])
            if e_ == 0:
                for i in range(8):
                    sl = slice(i * 512, (i + 1) * 512)
                    b = rot.next()
                    for kc in range(8):
                        P.op("pe", I("matmul", PS[b], lhsT=wir[:, ws, kc, 0, :], rhs=hTa[:, kc, sl], start=(kc == 0), stop=(kc == 7)),
                             reads=[("wir", ws)], writes=[psk(b)], inc=(kc == 7))
                    P.op("act", I("activation", out=Gb[:, sl], in_=PS[b], func=AF.Gelu_apprx_tanh), writes=[psk(b), ("Gb", i // 2)])
        for sg in range(4):
            o = slice(sg * SEG, (sg + 1) * SEG)
            P.op("dve", I("scalar_tensor_tensor", out=Hf[:, o], in0=Hf[:, o], scalar=1.0, in1=Ib[:, o], op0=ALU.mult, op1=ALU.add), reads=[("Ib", sg)], writes=[("Hf", sg)])
            P.op("dve", I("scalar_tensor_tensor", out=xbb[:, o], in0=Hf[:, o], scalar=1.0, in1=Gb[:, o], op0=ALU.mult, op1=ALU.mult), reads=[("Hf", sg), ("Gb", sg)], writes=[("xbb", sg)])
        P.dma("sp", I("dma_start", out=yrec[c * 128:(c + 1) * 128, :], in_=xbb), reads=[("xbb", g) for g in range(4)], writes=[("yrec", c)])
    P.barrier()
    AR.pop()
    if stop_after == "B1":
        return finish(nc, P, AR, dbg, locals())

    fused_tiles(1, yrec, "w_or", xTb, None, True)
    return finish(nc, P, AR, dbg, locals())


def finish(nc, P, AR, dbg, loc):
    if dbg:
        for nm in dbg:
            src = loc[nm]
            shp = list(src.shape)
            o = nc.dram_tensor("dbg_" + nm, shp, F32, kind="ExternalOutput").ap()
            rows = shp[0]
            for r0 in range(0, rows, 128):
                r1 = min(rows, r0 + 128)
                P.dma("pool", I("dma_start", out=o[r0:r1], in_=src[r0:r1]), writes=[("dbg", nm, r0)])
    alld = []
    for e in P.E.values():
        if e.count > 0:
            alld.append(Dep(e.name + "_b", e.sem, e.count))
        for slot in e.ring:
            if slot[1] > 0:
                alld.append(DmaDep(e.name + "_b", slot[0], slot[1]))
    for e in P.E.values():
        waits = P._waits_for(e, alld)
        if waits:
            e.ops.append((waits, None, None, 0))
    P.emit()
    return nc


def kernel(**inputs):
    maps = host_prep(inputs)
    nc = build()
    res = run_bass_kernel_spmd(nc, maps, core_ids=list(range(8)))
    ys = [np.asarray(r["y"], np.float32) for r in res.results]
    y_prompt = np.stack(ys[0:4], 0).reshape(8, 2048, 1024)
    y_sample = np.stack(ys[4:8], 0)
    return (y_prompt, y_sample)
```

```python
import numpy as np
import concourse.bass as bass
import concourse.mybir as mybir
from concourse.bass_utils import run_bass_kernel_spmd

F32 = mybir.dt.float32
BF16 = mybir.dt.bfloat16
ALU = mybir.AluOpType
AF = mybir.ActivationFunctionType

SAME_ENGINE_SYNC = True
SEM_LIMIT = 60000
DMA_RING = 12


class Dep:
    __slots__ = ("eng", "sem", "val")

    def __init__(self, eng, sem, val):
        self.eng = eng
        self.sem = sem
        self.val = val


class EngState:
    def __init__(self, name):
        self.name = name
        self.ops = []
        self.sem = None
        self.count = 0
        self.pending = []
        self.waited = {}
        self.ring = []
        self.ring_pos = 0
        self.last = None


class Prog:
    ENGS = ("pe", "act", "dve", "pool", "sp")

    def __init__(self, nc, nsems=100):
        self.nc = nc
        self.sem_pool = [nc.alloc_semaphore(name=f"s{i}") for i in range(nsems)]
        self.sem_idx = 0
        self.E = {n: EngState(n) for n in self.ENGS}
        for e in self.E.values():
            e.sem = self.new_sem()
        for q in ("sp", "pool", "act"):
            self.E[q].ring = [[self.new_sem(), 0] for _ in range(DMA_RING)]
        self.res = {}
        self.dma_deps_outstanding = []

    def new_sem(self):
        s = self.sem_pool[self.sem_idx]
        self.sem_idx += 1
        return s

    def _collect(self, eng, reads, writes):
        deps = []
        for r in reads:
            st = self.res.get(r)
            if st and st[0] is not None:
                deps.append(st[0])
        for w in writes:
            st = self.res.get(w)
            if st:
                if st[0] is not None:
                    deps.append(st[0])
                deps.extend(st[1])
        return deps

    def _waits_for(self, e, deps, is_dma=False):
        need = {}
        for d in deps:
            if d.eng == e.name and not isinstance(d, DmaDep) and not is_dma:
                if e.name == "pe" or not SAME_ENGINE_SYNC:
                    continue
            assert d.val is not None, f"dependency on pending (non-inc) op of {d.eng}"
            k = id(d.sem)
            if k not in need or need[k][1] < d.val:
                need[k] = (d.sem, d.val)
        waits = []
        for k, (sem, val) in need.items():
            if e.waited.get(k, 0) >= val:
                continue
            e.waited[k] = val
            waits.append((sem, val))
        return waits

    def _mark(self, dep, reads, writes):
        for r in reads:
            st = self.res.setdefault(r, [None, []])
            st[1].append(dep)
        for w in writes:
            self.res[w] = [dep, []]

    def op(self, eng, fn, reads=(), writes=(), inc=True):
        e = self.E[eng]
        deps = self._collect(eng, reads, writes)
        waits = self._waits_for(e, deps)
        if inc:
            if e.count >= SEM_LIMIT:
                e.sem = self.new_sem()
                e.count = 0
            e.count += 1
            dep = Dep(eng, e.sem, e.count)
            for p in e.pending:
                p.sem = e.sem
                p.val = e.count
            e.pending = []
            e.ops.append((waits, fn, e.sem, 1))
        else:
            dep = Dep(eng, None, None)
            e.pending.append(dep)
            e.ops.append((waits, fn, None, 0))
        e.last = dep
        self._mark(dep, reads, writes)
        return dep

    def dma(self, q, fn, reads=(), writes=()):
        e = self.E[q]
        deps = self._collect(q, reads, writes)
        slot = e.ring[e.ring_pos % DMA_RING]
        e.ring_pos += 1
        if slot[1] >= SEM_LIMIT - 16:
            slot[0] = self.new_sem()
            slot[1] = 0
            prev = None
        else:
            prev = DmaDep(q, slot[0], slot[1]) if slot[1] > 0 else None
        if prev is not None:
            deps.append(prev)
        waits = self._waits_for(e, deps, is_dma=True)
        slot[1] += 16
        dep = DmaDep(q, slot[0], slot[1])
        e.ops.append((waits, fn, slot[0], 16))
        self._mark(dep, reads, writes)
        self.dma_deps_outstanding.append(dep)
        return dep

    def barrier(self):
        alld = []
        for e in self.E.values():
            assert not e.pending, f"pending non-inc ops on {e.name} at barrier"
            if e.count > 0:
                alld.append(Dep(e.name + "_b", e.sem, e.count))
            if e.name == "pool":
                continue
            for slot in e.ring:
                if slot[1] > 0:
                    alld.append(DmaDep(e.name + "_b", slot[0], slot[1]))
        for e in self.E.values():
            waits = self._waits_for(e, alld)
            if waits:
                e.ops.append((waits, None, None, 0))
        keep = {}
        for k, st in self.res.items():
            lw = st[0] if (isinstance(st[0], DmaDep) and st[0].eng == "pool") else None
            rd = [d for d in st[1] if isinstance(d, DmaDep) and d.eng == "pool"]
            if lw is not None or rd:
                keep[k] = [lw, rd]
        self.res = keep

    def emit(self):
        nc = self.nc
        E = self.E

        def run(engobj, st):
            for waits, fn, sem, amt in st.ops:
                for (s, v) in waits:
                    engobj.wait_ge(s, v)
                if fn is not None:
                    inst = fn(engobj)
                    if sem is not None:
                        inst.then_inc(sem, amt)

        with nc.Block() as block:
            @block.tensor
            def _(x):
                run(x, E["pe"])

            @block.scalar
            def _(x):
                run(x, E["act"])

            @block.vector
            def _(x):
                run(x, E["dve"])

            @block.gpsimd
            def _(x):
                run(x, E["pool"])

            @block.sync
            def _(x):
                run(x, E["sp"])

    def stats(self):
        return {n: len(e.ops) for n, e in self.E.items()}, self.sem_idx


class DmaDep(Dep):
    __slots__ = ()


class Arena:
    def __init__(self, ap_f32, nbytes):
        self.base = ap_f32
        self.n = nbytes
        self.off = 0
        self.marks = []

    def push(self):
        self.marks.append(self.off)

    def pop(self):
        self.off = self.marks.pop()

    def alloc(self, free_shape, dtype, parts=128):
        esz = 4 if dtype == F32 else 2
        nel = int(np.prod(free_shape))
        nb = (nel * esz + 63) // 64 * 64
        assert self.off + nb <= self.n, f"arena overflow {self.off}+{nb}>{self.n}"
        a = self.base[:, self.off // 4:(self.off + nb) // 4]
        self.off += nb
        if dtype != F32:
            a = a.bitcast(dtype)
        a = a[:, 0:nel]
        if len(free_shape) == 2:
            a = a.rearrange("p (a b) -> p a b", a=free_shape[0])
        elif len(free_shape) == 3:
            a = a.rearrange("p (a b c) -> p a b c", a=free_shape[0], b=free_shape[1])
        elif len(free_shape) == 4:
            a = a.rearrange("p (a b c d) -> p a b c d", a=free_shape[0], b=free_shape[1], c=free_shape[2])
        return a


def I(method, *args, **kw):
    def fn(e):
        return getattr(e, method)(*args, **kw)
    return fn


NT = 4096
D = 1024
A2_HEADS = 8
A2_QB = 8
A2_S = 7
NEG = -30000.0
EPS = 1e-6
FT_TILES = [410, 410, 410, 410, 408] * 2
ARENA_BYTES = 206 * 1024

PV = {}
_pvo = 0


def _pv_add(name, w):
    global _pvo
    PV[name] = (_pvo, w)
    _pvo += w


for _n, _w in [("g_mix_pre", 16), ("g_mix_post", 16), ("g_ffn_pre", 16), ("g_ffn_post", 16), ("q_norm", 3), ("kv_norm", 2),
               ("rc_w", 32), ("rc_b", 8), ("ga_b", 16), ("gx_b", 16), ("lam", 16), ("fc_w", 384), ("fc_b", 128),
               ("keep", 1), ("eps", 1), ("one", 1), ("quarter", 1)]:
    _pv_add(_n, _w)
NPV = _pvo

WSHAPES = {
    "w_in": [128, 8 * 2368], "w_uq": [128, 3 * 8 * 192], "w_ukv": [128, 2 * 1024], "w_oa": [128, 8 * 1024],
    "w_ir": [128, 8 * 2048], "w_or": [128, 8 * 1024], "w_g": [128, 2 * 2 * 8 * 128],
    "w_up": [2 * 8 * 128, 8192], "w_dn": [2 * 8 * 128, 4096],
}


def cv(vec):
    return np.ascontiguousarray(np.asarray(vec, np.float32).reshape(-1, 128).T)


def host_prep(inp):
    f = lambda k: np.asarray(inp[k], np.float32)
    pvb = np.zeros((128, NPV), np.float32)

    def put(name, arr):
        o, w = PV[name]
        assert arr.shape == (128, w), (name, arr.shape, w)
        pvb[:, o:o + w] = arr

    for nm, key in [("g_mix_pre", "norm_mix_pre"), ("g_mix_post", "norm_mix_post"), ("g_ffn_pre", "norm_ffn_pre"), ("g_ffn_post", "norm_ffn_post")]:
        put(nm, np.concatenate([cv(f(key)[l]) for l in range(2)], axis=1))
    put("q_norm", cv(f("q_norm")[0]))
    put("kv_norm", cv(f("kv_norm")[0]))
    put("rc_w", np.concatenate([cv(f("conv_w_rec")[0, k]) for k in range(4)], axis=1))
    put("rc_b", cv(f("conv_b_rec")[0]))
    put("ga_b", np.concatenate([cv(f("gate_a_b")[0, e]) for e in range(2)], axis=1))
    put("gx_b", np.concatenate([cv(f("gate_x_b")[0, e]) for e in range(2)], axis=1))
    put("lam", np.concatenate([cv(f("lru_lambda")[0, e]) for e in range(2)], axis=1))
    put("fc_w", np.concatenate([cv(f("conv_w_ffn")[l, k]) for l in range(2) for k in range(3)], axis=1))
    put("fc_b", np.concatenate([cv(f("conv_b_ffn")[l]) for l in range(2)], axis=1))
    put("eps", np.full((128, 1), EPS, np.float32))
    put("one", np.ones((128, 1), np.float32))
    put("quarter", np.full((128, 1), 0.25, np.float32))

    def lhsT(w):
        K, N = w.shape
        return np.ascontiguousarray(w.reshape(K // 128, 128, N).transpose(1, 0, 2))

    wi = f("w_in_attn")[0]
    z96 = np.zeros((1024, 64), np.float32)
    kpeA = np.concatenate([z96, wi[:, 640:672]], axis=1)
    kpeB = np.concatenate([z96, wi[:, 656:672], wi[:, 640:656]], axis=1)
    w_in = np.concatenate([wi[:, 0:640], kpeA, kpeB, wi[:, 672:2208]], axis=1)
    assert w_in.shape[1] == 2368
    wq = f("w_uq")[0].reshape(384, 8, 96)
    wqB = np.concatenate([wq[:, :, 0:64], wq[:, :, 80:96], wq[:, :, 64:80]], axis=2)
    w_uq = np.concatenate([wq, wqB], axis=2).reshape(384, 8 * 192)
    gaw = f("gate_a_w")[0]
    gxw = f("gate_x_w")[0]
    w_g = np.stack([gaw, gxw], axis=0).transpose(3, 0, 1, 2, 4)
    wup = f("w_ffn_up")
    wup_r = wup.reshape(2, 8, 128, 2, 8, 4, 128)
    wup_l = np.ascontiguousarray(wup_r.transpose(0, 4, 2, 5, 1, 3, 6)).reshape(2 * 8 * 128, 8192)
    wdn = f("w_ffn_down")
    wdn_r = wdn.reshape(2, 32, 128, 8, 128)
    wdn_l = np.ascontiguousarray(wdn_r.transpose(0, 3, 2, 1, 4)).reshape(2 * 8 * 128, 4096)
    shared = {
        "w_in": lhsT(w_in).reshape(128, -1), "w_uq": lhsT(w_uq).reshape(128, -1), "w_ukv": lhsT(f("w_ukv")[0]).reshape(128, -1),
        "w_oa": lhsT(f("w_out_attn")[0]).reshape(128, -1), "w_ir": lhsT(f("w_in_rec")[0]).reshape(128, -1),
        "w_or": lhsT(f("w_out_rec")[0]).reshape(128, -1), "w_g": np.ascontiguousarray(w_g).reshape(128, -1),
        "w_up": wup_l, "w_dn": wdn_l, "ident": np.eye(128, dtype=np.float32),
    }
    for k, shp in WSHAPES.items():
        assert list(shared[k].shape) == shp, (k, shared[k].shape, shp)
    rpb = f("na_rpb")[0]
    p = np.arange(128)
    kc = p % 64
    half = p // 64
    qc = np.arange(64)
    cs = np.clip(qc - 8, 0, 48)
    colok = (kc[:, None] >= cs[None, :]) & (kc[:, None] < cs[None, :] + 16)
    cidx = np.clip(kc[:, None] - qc[None, :] + 15, 0, 30)
    t2 = np.zeros((128, 8, 22, 64), np.float32)
    for i in range(22):
        dr = 10 - i + half
        rowok = np.abs(dr) <= 7
        ridx = np.clip(dr + 7, 0, 14)
        for h in range(8):
            v = rpb[h][ridx[:, None], cidx]
            v = np.where(rowok[:, None], v, 0.0)
            t2[:, h, i, :] = np.where(colok, v, NEG)
    shared["t2"] = t2.reshape(128, -1)

    xs = [np.ascontiguousarray(f("x_prompt")[2 * i:2 * i + 2].reshape(NT, D)) for i in range(4)] + \
         [np.ascontiguousarray(f("x_sample")[i]) for i in range(4)]
    maps = []
    t = np.arange(NT)
    for core in range(8):
        prompt = core < 4
        slen = 2048 if prompt else 4096
        seq = t // slen
        pos = (t % slen).astype(np.float32)
        inv = (10000.0 ** (-np.arange(0, 32, 2, dtype=np.float32) / 32)).astype(np.float32)
        ang = (pos[:, None] * inv[None, :]).astype(np.float32)
        c, s = np.cos(ang).T.astype(np.float32), np.sin(ang).T.astype(np.float32)
        rope = np.stack([np.concatenate([c, c], 0), np.concatenate([-s, s], 0)], 0)
        mlak = np.stack([(seq == 0), (seq == 1)], 0).astype(np.float32)
        mlaq = np.where(mlak > 0, 0.0, NEG).astype(np.float32)
        gr = t // 64
        nak = (gr[None, :] % 16 == np.arange(16)[:, None]).astype(np.float32)
        rows_seq = 32 if prompt else 64
        so = (gr // rows_seq) * rows_seq
        rs = np.clip(gr - so - 4, 0, rows_seq - 8) + so
        naq = np.full((16, NT), NEG, np.float32)
        for d in range(8):
            naq[(rs + d) % 16, t] = 0.0
        pvc = pvb.copy()
        pvc[:, PV["keep"][0]] = 0.0 if prompt else 1.0
        m = dict(shared)
        m.update({"x": xs[core], "pv": pvc, "rope": np.ascontiguousarray(rope), "mlak": mlak, "mlaq": mlaq, "nak": nak, "naq": naq})
        maps.append(m)
    return maps


def build(stop_after=None, dbg=False):
    nc = bass.Bass("TRN2", target_bir_lowering=False)
    din = lambda n, s: nc.dram_tensor(n, s, F32, kind="ExternalInput").ap()
    x_d = din("x", [NT, D])
    pv_d = din("pv", [128, NPV])
    rope_d = din("rope", [2, 32, NT])
    mlak_d, mlaq_d = din("mlak", [2, NT]), din("mlaq", [2, NT])
    nak_d, naq_d = din("nak", [16, NT]), din("naq", [16, NT])
    t2_d = din("t2", [128, 8 * 22 * 64])
    ident_d = din("ident", [128, 128])
    wf = {k: din(k, s) for k, s in WSHAPES.items()}
    y_d = nc.dram_tensor("y", [NT, D], F32, kind="ExternalOutput").ap()
    scr = lambda n, s, dt: nc.dram_tensor(n, s, dt, kind="Internal").ap()
    wb = {k: scr(k + "_b", s, BF16) for k, s in WSHAPES.items()}
    xTa, xTb = scr("xTa", [D, NT], F32), scr("xTb", [D, NT], F32)
    qna, kna = scr("qna", [512, NT], BF16), scr("kna", [512, NT], BF16)
    vna = scr("vna", [NT, 512], BF16)
    ao = scr("ao", [D, NT], BF16)
    yrec = scr("yrec", [D, NT], BF16)

    sbh = nc.alloc_sbuf_tensor("arena", [128, ARENA_BYTES // 4], F32)
    AR = Arena(sbh[:], ARENA_BYTES)
    PS = [nc.alloc_psum_tensor(f"ps{i}", [128, 512], F32)[:] for i in range(8)]
    P = Prog(nc, nsems=100)

    class Rot:
        def __init__(self, banks):
            self.banks = banks
            self.i = 0

        def next(self):
            b = self.banks[self.i % len(self.banks)]
            self.i += 1
            return b

    def psk(b):
        return ("ps", b)

    WKEYS = {}

    def cast_w(name):
        rows, L = WSHAPES[name]
        step = 8192 if L % 8192 == 0 or L > 8192 else L
        for r0 in range(0, rows, 128):
            c0 = 0
            while c0 < L:
                c1 = min(L, c0 + step)
                P.dma("pool", I("dma_start", out=wb[name][r0:r0 + 128, c0:c1], in_=wf[name][r0:r0 + 128, c0:c1]),
                      writes=[("W", name, r0, c0)])
                WKEYS.setdefault((name, r0), []).append(("W", name, r0, c0))
                c0 = c1

    for nm in ["w_in", "w_uq", "w_ukv"]:
        cast_w(nm)

    pv = AR.alloc([NPV], F32)
    ident = AR.alloc([128], F32)
    ones_b = AR.alloc([128], BF16)
    P.dma("sp", I("dma_start", out=pv, in_=pv_d), writes=["pv"])
    P.dma("sp", I("dma_start", out=ident, in_=ident_d), writes=["ident"])
    P.op("pool", I("memset", ones_b, 1.0), writes=["ones"])

    def pvc(name, i=0, w=1):
        o, _ = PV[name]
        return pv[:, o + i:o + i + w]

    eps_ap = pvc("eps")
    keep_ap = pvc("keep")

    def rstd_of(srcs, n, Dn, sq, sqkey, rstd, rstdkey, rot):
        C = len(srcs)
        for c, (ap, key, isps) in enumerate(srcs):
            if isps or c % 2 == 0:
                P.op("act", I("activation", out=sq[:, c, 0:n], in_=ap, func=AF.Square),
                     reads=[] if isps else [key], writes=[(sqkey, c)] + ([key] if isps else []))
            else:
                P.op("pool", I("tensor_tensor", out=sq[:, c, 0:n], in0=ap, in1=ap, op=ALU.mult),
                     reads=[key], writes=[(sqkey, c)])
        b = rot.next()
        for c in range(C):
            P.op("pe", I("matmul", PS[b][:, 0:n], lhsT=ones_b, rhs=sq[:, c, 0:n], start=(c == 0), stop=(c == C - 1)),
                 reads=["ones", (sqkey, c)], writes=[psk(b)], inc=(c == C - 1))
        P.op("act", I("activation", out=rstd[:, 0:n], in_=PS[b][:, 0:n], func=AF.Sqrt, scale=1.0 / Dn, bias=eps_ap),
             reads=["pv"], writes=[psk(b), rstdkey])
        P.op("dve", I("reciprocal", out=rstd[:, 0:n], in_=rstd[:, 0:n]), writes=[rstdkey])

    def sq_part(srcs, n, sq, sqkey, c0=0):
        for c_, (ap, key) in enumerate(srcs):
            c = c0 + c_
            if c % 2 == 0:
                P.op("act", I("activation", out=sq[:, c, 0:n], in_=ap, func=AF.Square), reads=[key], writes=[(sqkey, c)])
            else:
                P.op("pool", I("tensor_tensor", out=sq[:, c, 0:n], in0=ap, in1=ap, op=ALU.mult), reads=[key], writes=[(sqkey, c)])

    def stat_part(C, n, Dn, sq, sqkey, rstd, rstdkey, rot, c0=0):
        b = rot.next()
        for c in range(C):
            P.op("pe", I("matmul", PS[b][:, 0:n], lhsT=ones_b, rhs=sq[:, c0 + c, 0:n], start=(c == 0), stop=(c == C - 1)),
                 reads=["ones", (sqkey, c0 + c)], writes=[psk(b)], inc=(c == C - 1))
        P.op("act", I("activation", out=rstd[:, 0:n], in_=PS[b][:, 0:n], func=AF.Sqrt, scale=1.0 / Dn, bias=eps_ap),
             reads=["pv"], writes=[psk(b), rstdkey])
        P.op("dve", I("reciprocal", out=rstd[:, 0:n], in_=rstd[:, 0:n]), writes=[rstdkey])

    def load_w(dst, name, rows0, col0, ncols, key, q="sp"):
        P.dma(q, I("dma_start", out=dst, in_=wb[name][rows0:rows0 + 128, col0:col0 + ncols]),
              reads=WKEYS[(name, rows0)], writes=[key])

    AR.push()
    qn = AR.alloc([3, NT], BF16)
    kvn = AR.alloc([2, NT], BF16)
    kpe = AR.alloc([NT], BF16)
    AR.push()
    w_in = AR.alloc([8, 2368], BF16)
    xin2 = AR.alloc([2, 4, 1024], F32)
    xT = AR.alloc([8, 512], F32)
    sq = AR.alloc([8, 512], BF16)
    hT = AR.alloc([8, 512], BF16)
    rstd = AR.alloc([512], F32)
    rstd2 = AR.alloc([512], F32)
    lat = AR.alloc([5, 512], F32)
    ropet = AR.alloc([2, 2, 512], F32)
    tmpA = AR.alloc([512], F32)
    tmpB = AR.alloc([512], F32)
    stq = AR.alloc([4, 512], BF16)
    stk = AR.alloc([4, 512], BF16)
    stv = AR.alloc([4, 512], BF16)
    rot = Rot([0, 1, 2, 3, 4, 5, 6, 7])
    gpre = lambda l, c: pvc("g_mix_pre", l * 8 + c)
    xTa_v = xTa.rearrange("(c p) t -> p c t", p=128)
    xTb_v = xTb.rearrange("(c p) t -> p c t", p=128)
    def a1_front(i):
        t0 = i * 512
        xin = xin2[:, i % 2]
        if i + 1 < 8:
            P.dma("sp", I("dma_start", out=xin2[:, (i + 1) % 2], in_=x_d[t0 + 512:t0 + 1024, :].rearrange("(s p) d -> p s d", p=128)), writes=[("xin", (i + 1) % 2)])
        P.dma("sp", I("dma_start", out=ropet[64:96, i % 2], in_=rope_d[:, :, t0:t0 + 512].rearrange("a p t -> p a t")), writes=[("ropet", i % 2)])
        for c in range(8):
            b = rot.next()
            for s in range(4):
                P.op("pe", I("transpose", out=PS[b][:, s * 128:(s + 1) * 128], in_=xin[:, s, c * 128:(c + 1) * 128], identity=ident),
                     reads=[("xin", i % 2), "ident"], writes=[psk(b)], inc=(s == 3))
            if c % 2 == 0:
                P.op("act", I("activation", out=xT[:, c, :], in_=PS[b], func=AF.Copy), writes=[psk(b), ("xT", c)])
            else:
                P.op("dve", I("tensor_copy", out=xT[:, c, :], in_=PS[b]), writes=[psk(b), ("xT", c)])
        P.dma("sp", I("dma_start", out=xTa_v[:, :, t0:t0 + 512], in_=xT), reads=[("xT", c) for c in range(8)], writes=[("xTa", i)])
        sq_part([(xT[:, c, :], ("xT", c)) for c in range(8)], 512, sq, "sq")

    def a1_stat(i):
        stat_part(8, 512, 1024, sq, "sq", rstd, "rstd", rot)

    P.dma("sp", I("dma_start", out=xin2[:, 0], in_=x_d[0:512, :].rearrange("(s p) d -> p s d", p=128)), writes=[("xin", 0)])
    load_w(w_in.rearrange("p a b -> p (a b)"), "w_in", 0, 0, 8 * 2368, "w_in")
    a1_front(0)
    a1_stat(0)
    for i in range(8):
        t0 = i * 512
        for c in range(8):
            P.op("dve", I("scalar_tensor_tensor", out=hT[:, c, :], in0=xT[:, c, :], scalar=gpre(0, c), in1=rstd, op0=ALU.mult, op1=ALU.mult),
                 reads=[("xT", c), "rstd", "pv"], writes=[("hT", c)])
        hreads = [("hT", c) for c in range(8)] + ["w_in"]

        def zmm(col0, M, b):
            for kc in range(8):
                P.op("pe", I("matmul", PS[b][0:M, :], lhsT=w_in[:, kc, col0:col0 + M], rhs=hT[:, kc, :], start=(kc == 0), stop=(kc == 7)),
                     reads=hreads, writes=[psk(b)], inc=(kc == 7))

        for j in range(5):
            b = rot.next()
            zmm(j * 128, 128, b)
            P.op("act", I("activation", out=lat[:, j, :], in_=PS[b], func=AF.Copy), writes=[psk(b), ("lat", j)])
        sq_part([(lat[:, j, :], ("lat", j)) for j in range(5)], 512, sq, "sq")
        bA = rot.next()
        zmm(640, 96, bA)
        bB = rot.next()
        zmm(736, 96, bB)
        P.op("dve", I("tensor_tensor", out=tmpA[64:96, :], in0=PS[bA][64:96, :], in1=ropet[64:96, i % 2, 0, :], op=ALU.mult),
             reads=[("ropet", i % 2)], writes=[psk(bA), "tmpA"])
        P.op("dve", I("tensor_tensor", out=tmpB[64:96, :], in0=PS[bB][64:96, :], in1=ropet[64:96, i % 2, 1, :], op=ALU.mult),
             reads=[("ropet", i % 2)], writes=[psk(bB), "tmpB"])
        P.op("pool", I("tensor_tensor", out=kpe[64:96, t0:t0 + 512], in0=tmpA[64:96, :], in1=tmpB[64:96, :], op=ALU.add),
             reads=["tmpA", "tmpB"], writes=[("kpe", i)])
        for j in range(4):
            b = rot.next()
            zmm(832 + j * 128, 128, b)
            P.op("act", I("activation", out=stq[:, j, :], in_=PS[b], func=AF.Identity, scale=0.125), writes=[psk(b), ("stq", j)])
        P.dma("sp", I("dma_start", out=qna.rearrange("(j p) t -> p j t", p=128)[:, :, t0:t0 + 512], in_=stq), reads=[("stq", j) for j in range(4)], writes=[("qna", i)])
        stat_part(3, 512, 384, sq, "sq", rstd2, "rstd2", rot)
        for j in range(3):
            P.op("dve", I("scalar_tensor_tensor", out=qn[:, j, t0:t0 + 512], in0=lat[:, j, :], scalar=pvc("q_norm", j), in1=rstd2, op0=ALU.mult, op1=ALU.mult),
                 reads=[("lat", j), "rstd2", "pv"], writes=[("qn", i)])
        stat_part(2, 512, 256, sq, "sq", rstd2, "rstd2", rot, c0=3)
        for j in range(2):
            P.op("dve", I("scalar_tensor_tensor", out=kvn[:, j, t0:t0 + 512], in0=lat[:, 3 + j, :], scalar=pvc("kv_norm", j), in1=rstd2, op0=ALU.mult, op1=ALU.mult),
                 reads=[("lat", 3 + j), "rstd2", "pv"], writes=[("kvn", i)])
        for j in range(4):
            b = rot.next()
            zmm(1344 + j * 128, 128, b)
            P.op("dve", I("tensor_copy", out=stk[:, j, :], in_=PS[b]), writes=[psk(b), ("stk", j)])
        P.dma("sp", I("dma_start", out=kna.rearrange("(j p) t -> p j t", p=128)[:, :, t0:t0 + 512], in_=stk), reads=[("stk", j) for j in range(4)], writes=[("kna", i)])
        if i + 1 < 8:
            a1_front(i + 1)
        for s in range(4):
            b = rot.next()
            for kc in range(8):
                P.op("pe", I("matmul", PS[b], lhsT=hT[:, kc, s * 128:(s + 1) * 128], rhs=w_in[:, kc, 1856:2368], start=(kc == 0), stop=(kc == 7)),
                     reads=hreads, writes=[psk(b)], inc=(kc == 7))
            if s % 2 == 0:
                P.op("act", I("activation", out=stv[:, s, :], in_=PS[b], func=AF.Copy), writes=[psk(b), ("stv", s)])
            else:
                P.op("dve", I("tensor_copy", out=stv[:, s, :], in_=PS[b]), writes=[psk(b), ("stv", s)])
        P.dma("sp", I("dma_start", out=vna[t0:t0 + 512, :].rearrange("(s p) f -> p s f", p=128), in_=stv), reads=[("stv", s) for s in range(4)], writes=[("vna", i)])
        if i + 1 < 8:
            a1_stat(i + 1)
    P.barrier()
    AR.pop()
    if stop_after == "A1":
        return finish(nc, P, AR, dbg, locals())

    AR.push()
    w_uq = AR.alloc([3, 8, 192], BF16)
    w_ukv = AR.alloc([2, 1024], BF16)
    load_w(w_uq.rearrange("p a b c -> p (a b c)"), "w_uq", 0, 0, 3 * 8 * 192, "w_uq")
    load_w(w_ukv.rearrange("p a b -> p (a b)"), "w_ukv", 0, 0, 2048, "w_ukv")
    KT = AR.alloc([2, NT], BF16)
    VA = AR.alloc([2, 32, 128], BF16)
    QT = AR.alloc([2, NT], BF16)
    PT = AR.alloc([6, 512], BF16)
    ropef = AR.alloc([2, NT], F32)
    Rr = AR.alloc([2, 512], F32)
    ost = AR.alloc([8, 512], BF16)
    tA = AR.alloc([2, 512], F32)
    tB = AR.alloc([2, 512], F32)
    P.dma("sp", I("dma_start", out=ropef[64:96], in_=rope_d.rearrange("a p t -> p a t")), writes=["ropef"])
    for b2 in range(2):
        P.op("act", I("activation", out=KT[64:96, b2, :], in_=kpe[64:96, :], func=AF.Copy), writes=[("KTpe", b2)])
        P.dma("pool", I("dma_start", out=KT[96:98, b2, :], in_=mlak_d), writes=[("KTaug", b2)])
        P.dma("pool", I("dma_start", out=QT[96:98, b2, :], in_=mlaq_d), writes=[("QTaug", b2)])
        P.op("pool", I("memset", VA[:, b2, :, 64:128], 1.0), writes=[("VAone", b2)])
    for nm in ["w_oa", "w_up", "w_dn", "w_ir", "w_g", "w_or"]:
        cast_w(nm)
    rotS = Rot([0, 1, 2, 3, 4, 5])
    obank = Rot([6, 7])
    scale = 96.0 ** -0.5
    pt_i = 0

    def mla_prep_units(h):
        hb = h % 2
        units = []
        def k_unit(i):
            b = rotS.next()
            for kc in range(2):
                P.op("pe", I("matmul", PS[b][0:64, :], lhsT=w_ukv[:, kc, h * 128:h * 128 + 64], rhs=kvn[:, kc, i * 512:(i + 1) * 512], start=(kc == 0), stop=(kc == 1)),
                     reads=["w_ukv"], writes=[psk(b)], inc=(kc == 1))
            P.op("dve", I("tensor_copy", out=KT[0:64, hb, i * 512:(i + 1) * 512], in_=PS[b][0:64, :]), writes=[psk(b), ("KT", hb)])
        def v_unit(g):
            b = rotS.next()
            for s in range(8):
                tc = g * 8 + s
                for kc in range(2):
                    P.op("pe", I("matmul", PS[b][:, s * 64:(s + 1) * 64], lhsT=kvn[:, kc, tc * 128:(tc + 1) * 128], rhs=w_ukv[:, kc, h * 128 + 64:h * 128 + 128], start=(kc == 0), stop=(kc == 1)),
                         reads=["w_ukv"], writes=[psk(b)], inc=(kc == 1 and s == 7))
            P.op("dve", I("tensor_copy", out=VA[:, hb, g * 8:(g + 1) * 8, 0:64], in_=PS[b].rearrange("p (s f) -> p s f", f=64)),
                 writes=[psk(b), ("VA", hb)])
        def q_unit(i):
            sl = slice(i * 512, (i + 1) * 512)
            bA = rotS.next()
            for kc in range(3):
                P.op("pe", I("matmul", PS[bA][0:96, :], lhsT=w_uq[:, kc, h, 0:96], rhs=qn[:, kc, sl], start=(kc == 0), stop=(kc == 2)),
                     reads=["w_uq"], writes=[psk(bA)], inc=(kc == 2))
            bB = rotS.next()
            for kc in range(3):
                P.op("pe", I("matmul", PS[bB][0:96, :], lhsT=w_uq[:, kc, h, 96:192], rhs=qn[:, kc, sl], start=(kc == 0), stop=(kc == 2)),
                     reads=["w_uq"], writes=[psk(bB)], inc=(kc == 2))
            P.op("dve", I("tensor_copy", out=QT[0:64, hb, sl], in_=PS[bA][0:64, :]), writes=[psk(bA), ("QT", hb)])
            P.op("dve", I("tensor_tensor", out=tA[64:96, i % 2, :], in0=PS[bA][64:96, :], in1=ropef[64:96, 0, sl], op=ALU.mult),
                 reads=["ropef"], writes=[psk(bA), ("tA", i % 2)])
            P.op("dve", I("tensor_tensor", out=tB[64:96, i % 2, :], in0=PS[bB][64:96, :], in1=ropef[64:96, 1, sl], op=ALU.mult),
                 reads=["ropef"], writes=[psk(bB), ("tB", i % 2)])
            P.op("dve", I("tensor_tensor", out=QT[64:96, hb, sl], in0=tA[64:96, i % 2, :], in1=tB[64:96, i % 2, :], op=ALU.add),
                 reads=[("tA", i % 2), ("tB", i % 2)], writes=[("QT", hb)])
        for i in range(8):
            units.append((k_unit, i))
        for g in range(4):
            units.append((v_unit, g))
        for i in range(8):
            units.append((q_unit, i))
        return units

    for f_, a_ in mla_prep_units(0):
        f_(a_)
    kc_iter = 0
    for h in range(A2_HEADS):
        hb = h % 2
        pend = mla_prep_units(h + 1) if h + 1 < A2_HEADS else []
        kread = [("KT", hb), ("KTpe", hb), ("KTaug", hb), ("QT", hb), ("QTaug", hb)]
        for qb in range(A2_QB):
            qs = slice(qb * 512, (qb + 1) * 512)
            ob = obank.next()
            sb_of = {}

            def qk(kc):
                b = rotS.next()
                sb_of[kc] = b
                P.op("pe", I("matmul", PS[b], lhsT=KT[0:98, hb, kc * 128:(kc + 1) * 128], rhs=QT[0:98, hb, qs], start=True, stop=True),
                     reads=kread, writes=[psk(b)])

            qk(0)
            qk(1)
            for kc in range(32):
                if kc + 2 < 32:
                    qk(kc + 2)
                kc_iter += 1
                if pend and kc_iter % 12 == 6:
                    f_, a_ = pend.pop(0)
                    f_(a_)
                b = sb_of[kc]
                slot = pt_i % 6
                pt_i += 1
                P.op("act", I("activation", out=PT[:, slot, :], in_=PS[b], func=AF.Exp, scale=scale), writes=[psk(b), ("PT", slot)])
                P.op("pe", I("matmul", PS[ob], lhsT=VA[:, hb, kc, :], rhs=PT[:, slot, :], start=(kc == 0), stop=(kc == 31)),
                     reads=[("PT", slot), ("VA", hb), ("VAone", hb)], writes=[psk(ob)], inc=(kc == 31))
            r = qb % 2
            P.op("dve", I("reciprocal", out=Rr[64:128, r, :], in_=PS[ob][64:128, :]), writes=[psk(ob), ("Rr", r)])
            P.op("dve", I("tensor_tensor", out=ost[0:64, qb, :], in0=PS[ob][0:64, :], in1=Rr[64:128, r, :], op=ALU.mult),
                 reads=[("Rr", r)], writes=[psk(ob), ("ost", qb)])
            P.dma("sp", I("dma_start", out=ao[h * 64:(h + 1) * 64, qs], in_=ost[0:64, qb, :]), reads=[("ost", qb)], writes=[("ao", h, qb)])
        while pend:
            f_, a_ = pend.pop(0)
            f_(a_)
    P.barrier()
    AR.pop()
    AR.pop()
    if stop_after == "A2":
        return finish(nc, P, AR, dbg, locals())

    AR.push()
    T2 = AR.alloc([8, 22, 64], BF16)
    identb = AR.alloc([128], BF16)
    KN = AR.alloc([2, NT], BF16)
    QN = AR.alloc([2, NT], BF16)
    VN = AR.alloc([2, 32, 128], BF16)
    PT = AR.alloc([4, 512], BF16)
    Rr = AR.alloc([2, 512], F32)
    ost = AR.alloc([8, 512], BF16)
    P.dma("pool", I("dma_start", out=T2.rearrange("p a b c -> p (a b c)"), in_=t2_d), writes=["T2"])
    P.op("dve", I("tensor_copy", out=identb, in_=ident), reads=["ident"], writes=["identb"])
    for b2 in range(2):
        P.dma("pool", I("dma_start", out=KN[64:80, b2, :], in_=nak_d), writes=[("KNaug", b2)])
        P.dma("pool", I("dma_start", out=QN[64:80, b2, :], in_=naq_d), writes=[("QNaug", b2)])
        P.op("pool", I("memset", VN[:, b2, :, 64:128], 1.0), writes=[("VNone", b2)])
    rotS = Rot([0, 1, 2, 3, 4, 5])
    obank = Rot([6, 7])
    vna_v = vna.rearrange("(c p) f -> p c f", p=128)
    st_i = 0
    def na_load(h):
        hb = h % 2
        P.dma("sp", I("dma_start", out=KN[0:64, hb, :], in_=kna[h * 64:(h + 1) * 64, :]), writes=[("KN", hb)])
        P.dma("sp", I("dma_start", out=QN[0:64, hb, :], in_=qna[h * 64:(h + 1) * 64, :]), writes=[("QN", hb)])
        for g in range(4):
            P.dma("sp", I("dma_start", out=VN[:, hb, g * 8:(g + 1) * 8, 0:64], in_=vna_v[:, g * 8:(g + 1) * 8, h * 64:(h + 1) * 64]), writes=[("VN", hb, g)])

    for h in range(8):
        hb = h % 2
        if h == 0:
            na_load(0)
        if h + 1 < 8:
            na_load(h + 1)
        kread = [("KN", hb), ("KNaug", hb), ("QN", hb), ("QNaug", hb)]
        for qb in range(8):
            qs = slice(qb * 512, (qb + 1) * 512)
            ob = obank.next()
            cl = list(range(max(0, 4 * qb - 2), min(32, 4 * qb + 6)))
            sb_of = {}

            def qk(ci):
                c = cl[ci]
                b = rotS.next()
                sb_of[ci] = b
                i0 = 10 - (2 * c - 8 * qb)
                P.op("pe", I("matmul", PS[b], lhsT=KN[0:80, hb, c * 128:(c + 1) * 128], rhs=QN[0:80, hb, qs], start=True, stop=False),
                     reads=kread, writes=[psk(b)], inc=False)
                P.op("pe", I("matmul", PS[b], lhsT=identb, rhs=T2[:, h, i0:i0 + 8, :].rearrange("p a b -> p (a b)"), start=False, stop=True),
                     reads=["identb", "T2"], writes=[psk(b)])

            qk(0)
            qk(1)
            for ci, c in enumerate(cl):
                if ci + 2 < len(cl):
                    qk(ci + 2)
                b = sb_of[ci]
                slot = st_i % 4
                st_i += 1
                P.op("act", I("activation", out=PT[:, slot, :], in_=PS[b], func=AF.Exp), writes=[psk(b), ("PT", slot)])
                P.op("pe", I("matmul", PS[ob], lhsT=VN[:, hb, c, :], rhs=PT[:, slot, :], start=(ci == 0), stop=(ci == len(cl) - 1)),
                     reads=[("PT", slot), ("VN", hb, c // 8), ("VNone", hb)], writes=[psk(ob)], inc=(ci == len(cl) - 1))
            r = qb % 2
            P.op("dve", I("reciprocal", out=Rr[64:128, r, :], in_=PS[ob][64:128, :]), writes=[psk(ob), ("Rr", r)])
            P.op("dve", I("tensor_tensor", out=ost[0:64, qb, :], in0=PS[ob][0:64, :], in1=Rr[64:128, r, :], op=ALU.mult),
                 reads=[("Rr", r)], writes=[psk(ob), ("ost", qb)])
            P.dma("sp", I("dma_start", out=ao[512 + h * 64:512 + (h + 1) * 64, qs], in_=ost[0:64, qb, :]), reads=[("ost", qb)], writes=[("ao", 8 + h, qb)])
    P.barrier()
    AR.pop()
    if stop_after == "A3":
        return finish(nc, P, AR, dbg, locals())

    def fused_tiles(l, src_d, wo_name, xin_d, xout_d, final):
        AR.push()
        w_o = AR.alloc([8, 1024], BF16)
        load_w(w_o.rearrange("p a b -> p (a b)"), wo_name, 0, 0, 8192, "w_o")
        NM = 412
        srcT = AR.alloc([8, NM], BF16)
        xTt = AR.alloc([8, NM], F32)
        fT = AR.alloc([8, 410], F32)
        x1d = AR.alloc([2, 8, NM], F32)
        sqt = AR.alloc([8, NM], BF16)
        sqe = AR.alloc([8, 410], BF16)
        rs1 = AR.alloc([NM], F32)
        rs2 = AR.alloc([NM], F32)
        rs3 = AR.alloc([NM], F32)
        h2 = AR.alloc([8, NM], BF16)
        aT = AR.alloc([32, 410], BF16)
        wup = AR.alloc([2, 4, 8, 256], BF16)
        wdn = AR.alloc([2, 32, 128], BF16)
        t1 = AR.alloc([4, 410], F32)
        ug = AR.alloc([2, 410], F32)
        uv = AR.alloc([2, 410], F32)
        gg = AR.alloc([2, 410], F32)
        yst = AR.alloc([2, 1024], F32) if final else None
        src_v = src_d.rearrange("(c p) t -> p c t", p=128)
        xin_v = xin_d.rearrange("(c p) t -> p c t", p=128)
        xout_v = None if final else xout_d.rearrange("(c p) t -> p c t", p=128)
        rot = Rot([0, 1, 2, 3, 4, 5, 6, 7])
        gpost = lambda c: pvc("g_mix_post", l * 8 + c)
        gfpre = lambda c: pvc("g_ffn_pre", l * 8 + c)
        gfpost = lambda c: pvc("g_ffn_post", l * 8 + c)
        fcw = lambda k, ch: pvc("fc_w", (l * 3 + k) * 64 + ch)
        fcb = lambda ch: pvc("fc_b", l * 64 + ch)
        cnt = {"wup": 0, "wdn": 0}

        def issue_wup(jg, slot):
            P.dma("sp", I("dma_start", out=wup[:, slot].rearrange("p a b c -> p (a b c)"), in_=wb["w_up"][(l * 8 + jg) * 128:(l * 8 + jg + 1) * 128, :]),
                  reads=WKEYS[("w_up", (l * 8 + jg) * 128)], writes=[("wup", slot)])

        def issue_wdn(oc, slot):
            P.dma("sp", I("dma_start", out=wdn[:, slot].rearrange("p a b -> p (a b)"), in_=wb["w_dn"][(l * 8 + oc) * 128:(l * 8 + oc + 1) * 128, :]),
                  reads=WKEYS[("w_dn", (l * 8 + oc) * 128)], writes=[("wdn", slot)])

        tiles = []
        t0 = 0
        for ti, n in enumerate(FT_TILES):
            tiles.append((ti, t0, n))
            t0 += n
        h2all = [("h2", c) for c in range(8)]
        aall = [("aT", j) for j in range(32)]

        def P_s0(ti, t0, n):
            n2 = n + 2
            x1 = x1d[:, ti % 2]
            xk = ("x1", ti % 2)
            lo = max(t0 - 1, 0)
            hi = min(t0 + n + 1, NT)
            j0 = lo - (t0 - 1)
            j1 = j0 + (hi - lo)
            P.dma("sp", I("dma_start", out=srcT[:, :, j0:j1], in_=src_v[:, :, lo:hi]), writes=["srcT"])
            P.dma("sp", I("dma_start", out=xTt[:, :, j0:j1], in_=xin_v[:, :, lo:hi]), writes=["xTt"])
            if j0 > 0:
                P.op("pool", I("memset", srcT[:, :, 0:1], 0.0), writes=["srcT"])
                P.op("pool", I("memset", xTt[:, :, 0:1], 0.0), writes=["xTt"])
            if j1 < n2:
                P.op("pool", I("memset", srcT[:, :, j1:n2], 0.0), writes=["srcT"])
                P.op("pool", I("memset", xTt[:, :, j1:n2], 0.0), writes=["xTt"])
            issue_wup(0, cnt["wup"] % 2)
            for oc in range(8):
                b = rot.next()
                for kc in range(8):
                    P.op("pe", I("matmul", PS[b][:, 0:n2], lhsT=w_o[:, kc, oc * 128:(oc + 1) * 128], rhs=srcT[:, kc, 0:n2], start=(kc == 0), stop=(kc == 7)),
                         reads=["w_o", "srcT"], writes=[psk(b)], inc=(kc == 7))
                P.op("act", I("activation", out=x1[:, oc, 0:n2], in_=PS[b][:, 0:n2], func=AF.Copy), writes=[psk(b), (xk, oc)])
            sq_part([(x1[:, c, 0:n2], (xk, c)) for c in range(8)], n2, sqt, "sqt")

        def P_s1(ti, t0, n):
            n2 = n + 2
            x1 = x1d[:, ti % 2]
            xk = ("x1", ti % 2)
            stat_part(8, n2, 1024, sqt, "sqt", rs1, "rs1", rot)
            for c in range(8):
                P.op("dve", I("scalar_tensor_tensor", out=x1[:, c, 0:n2], in0=x1[:, c, 0:n2], scalar=gpost(c), in1=rs1[:, 0:n2], op0=ALU.mult, op1=ALU.mult),
                     reads=["rs1", "pv"], writes=[(xk, c)])
                P.op("pool", I("tensor_tensor", out=x1[:, c, 0:n2], in0=x1[:, c, 0:n2], in1=xTt[:, c, 0:n2], op=ALU.add),
                     reads=["xTt"], writes=[(xk, c)])
            sq_part([(x1[:, c, 0:n2], (xk, c)) for c in range(8)], n2, sqt, "sqt")

        def P_s2(ti, t0, n):
            n2 = n + 2
            x1 = x1d[:, ti % 2]
            xk = ("x1", ti % 2)
            stat_part(8, n2, 1024, sqt, "sqt", rs2, "rs2", rot)
            for c in range(8):
                P.op("dve", I("scalar_tensor_tensor", out=h2[:, c, 0:n2], in0=x1[:, c, 0:n2], scalar=gfpre(c), in1=rs2[:, 0:n2], op0=ALU.mult, op1=ALU.mult),
                     reads=[(xk, c), "rs2", "pv"], writes=[("h2", c)])
            if t0 == 0:
                P.op("pool", I("memset", h2[:, :, 0:1], 0.0), writes=h2all)
            elif t0 == 2048:
                P.op("pool", I("tensor_scalar", out=h2[:, :, 0:1], in0=h2[:, :, 0:1], scalar1=keep_ap, scalar2=None, op0=ALU.mult), reads=["pv"], writes=h2all)
            if t0 + n == NT:
                P.op("pool", I("memset", h2[:, :, n + 1:n + 2], 0.0), writes=h2all)
            elif t0 + n == 2048:
                P.op("pool", I("tensor_scalar", out=h2[:, :, n + 1:n + 2], in0=h2[:, :, n + 1:n + 2], scalar1=keep_ap, scalar2=None, op0=ALU.mult), reads=["pv"], writes=h2all)

        def U_group(ti, t0, n, jg):
            n2 = n + 2
            slot = cnt["wup"] % 2
            cnt["wup"] += 1
            if jg + 1 < 8:
                issue_wup(jg + 1, cnt["wup"] % 2)
            else:
                issue_wdn(0, cnt["wdn"] % 2)
            for jj in range(4):
                j = jg * 4 + jj
                bg = rot.next()
                for kc in range(8):
                    P.op("pe", I("matmul", PS[bg][:, 0:n2], lhsT=wup[:, slot, jj, kc, 0:128], rhs=h2[:, kc, 0:n2], start=(kc == 0), stop=(kc == 7)),
                         reads=h2all + [("wup", slot)], writes=[psk(bg)], inc=(kc == 7))
                bv = rot.next()
                for kc in range(8):
                    P.op("pe", I("matmul", PS[bv][:, 0:n2], lhsT=wup[:, slot, jj, kc, 128:256], rhs=h2[:, kc, 0:n2], start=(kc == 0), stop=(kc == 7)),
                         reads=h2all + [("wup", slot)], writes=[psk(bv)], inc=(kc == 7))
                q = j % 2
                for (bb, ch, tq, uo, ukey) in ((bg, j, 2 * q, ug, "ug"), (bv, 32 + j, 2 * q + 1, uv, "uv")):
                    P.op("act", I("activation", out=t1[:, tq, 0:n], in_=PS[bb][:, 1:n + 1], func=AF.Identity, scale=fcw(1, ch), bias=fcb(ch)),
                         reads=["pv"], writes=[psk(bb), ("t1", tq)])
                    P.op("dve", I("scalar_tensor_tensor", out=t1[:, tq, 0:n], in0=PS[bb][:, 0:n], scalar=fcw(0, ch), in1=t1[:, tq, 0:n], op0=ALU.mult, op1=ALU.add),
                         reads=["pv"], writes=[psk(bb), ("t1", tq)])
                    P.op("dve", I("scalar_tensor_tensor", out=uo[:, q, 0:n], in0=PS[bb][:, 2:n + 2], scalar=fcw(2, ch), in1=t1[:, tq, 0:n], op0=ALU.mult, op1=ALU.add),
                         reads=["pv", ("t1", tq)], writes=[psk(bb), (ukey, q)])
                P.op("act", I("activation", out=gg[:, q, 0:n], in_=ug[:, q, 0:n], func=AF.Gelu_apprx_tanh), reads=[("ug", q)], writes=[("gg", q)])
                P.op("pool", I("tensor_tensor", out=aT[:, j, 0:n], in0=gg[:, q, 0:n], in1=uv[:, q, 0:n], op=ALU.mult),
                     reads=[("gg", q), ("uv", q)], writes=[("aT", j)])

        def D_oc(ti, t0, n, oc):
            slot = cnt["wdn"] % 2
            cnt["wdn"] += 1
            if oc + 1 < 8:
                issue_wdn(oc + 1, cnt["wdn"] % 2)
            b = rot.next()
            for fc in range(32):
                P.op("pe", I("matmul", PS[b][:, 0:n], lhsT=wdn[:, slot, fc, :], rhs=aT[:, fc, 0:n], start=(fc == 0), stop=(fc == 31)),
                     reads=aall + [("wdn", slot)], writes=[psk(b)], inc=(fc == 31))
            P.op("act", I("activation", out=fT[:, oc, 0:n], in_=PS[b][:, 0:n], func=AF.Copy), writes=[psk(b), ("fT", oc)])

        def E_e0(ti, t0, n):
            sq_part([(fT[:, c, 0:n], ("fT", c)) for c in range(8)], n, sqe, "sqe")

        def E_e1(ti, t0, n):
            x1 = x1d[:, ti % 2]
            xk = ("x1", ti % 2)
            stat_part(8, n, 1024, sqe, "sqe", rs3, "rs3", rot)
            for c in range(8):
                P.op("dve", I("scalar_tensor_tensor", out=fT[:, c, 0:n], in0=fT[:, c, 0:n], scalar=gfpost(c), in1=rs3[:, 0:n], op0=ALU.mult, op1=ALU.mult),
                     reads=["rs3", "pv"], writes=[("fT", c)])
                P.op("pool", I("tensor_tensor", out=x1[:, c, 1:n + 1], in0=x1[:, c, 1:n + 1], in1=fT[:, c, 0:n], op=ALU.add),
                     reads=[("fT", c)], writes=[(xk, c)])
            if not final:
                P.dma("sp", I("dma_start", out=xout_v[:, :, t0:t0 + n], in_=x1[:, :, 1:n + 1]), reads=[(xk, c) for c in range(8)], writes=[("xout", ti)])

        def E_e2(ti, t0, n):
            if not final:
                return
            x1 = x1d[:, ti % 2]
            xk = ("x1", ti % 2)
            s0 = 0
            si = 0
            while s0 < n:
                w = min(128, n - s0)
                ys = si % 2
                for half in range(2):
                    b = rot.next()
                    for c4 in range(4):
                        c = half * 4 + c4
                        P.op("pe", I("transpose", out=PS[b][0:w, c4 * 128:(c4 + 1) * 128], in_=x1[:, c, 1 + s0:1 + s0 + w], identity=ident),
                             reads=[(xk, c), "ident"], writes=[psk(b)], inc=(c4 == 3))
                    if half == 0:
                        P.op("act", I("activation", out=yst[0:w, ys, 0:512], in_=PS[b][0:w, :], func=AF.Copy), writes=[psk(b), ("yst", ys, 0)])
                    else:
                        P.op("dve", I("tensor_copy", out=yst[0:w, ys, 512:1024], in_=PS[b][0:w, :]), writes=[psk(b), ("yst", ys, 1)])
                P.dma("sp", I("dma_start", out=y_d[t0 + s0:t0 + s0 + w, :], in_=yst[0:w, ys, :]), reads=[("yst", ys, 0), ("yst", ys, 1)], writes=[("y", ti, si)])
                s0 += w
                si += 1

        P_s0(*tiles[0])
        P_s1(*tiles[0])
        P_s2(*tiles[0])
        for k, tl in enumerate(tiles):
            prev = tiles[k - 1] if k > 0 else None
            nxt = tiles[k + 1] if k + 1 < len(tiles) else None
            for jg in range(8):
                U_group(*tl, jg)
                if prev is not None and jg == 0:
                    E_e1(*prev)
                if prev is not None and jg == 2:
                    E_e2(*prev)
            if nxt is not None:
                P_s0(*nxt)
            D_oc(*tl, 0)
            D_oc(*tl, 1)
            if nxt is not None:
                P_s1(*nxt)
            D_oc(*tl, 2)
            D_oc(*tl, 3)
            if nxt is not None:
                P_s2(*nxt)
            for oc in range(4, 8):
                D_oc(*tl, oc)
            E_e0(*tl)
        E_e1(*tiles[-1])
        E_e2(*tiles[-1])
        P.barrier()
        AR.pop()

    fused_tiles(0, ao, "w_oa", xTa, xTb, False)
    if stop_after == "F0":
        return finish(nc, P, AR, dbg, locals())

    AR.push()
    hTa = AR.alloc([8, NT], BF16)
    AR.push()
    xTl = AR.alloc([2, 8, 512], F32)
    sq = AR.alloc([8, 512], BF16)
    rstd = AR.alloc([2, 512], F32)
    rot = Rot([0, 1, 2, 3, 4, 5, 6, 7])
    for i in range(8):
        t0 = i * 512
        s = i % 2
        P.dma("sp", I("dma_start", out=xTl[:, s], in_=xTb_v[:, :, t0:t0 + 512]), writes=[("xTl", s)])
        rstd_of([(xTl[:, s, c, :], ("xTl", s), False) for c in range(8)], 512, 1024, sq, "sq", rstd[:, s], ("rstd", s), rot)
        for c in range(8):
            P.op("dve", I("scalar_tensor_tensor", out=hTa[:, c, t0:t0 + 512], in0=xTl[:, s, c, :], scalar=pvc("g_mix_pre", 8 + c), in1=rstd[:, s], op0=ALU.mult, op1=ALU.mult),
                 reads=[("xTl", s), ("rstd", s), "pv"], writes=[("hTa", i)])
    P.barrier()
    AR.pop()
    wg = AR.alloc([2, 2, 8, 128], BF16)
    load_w(wg.rearrange("p a b c d -> p (a b c d)"), "w_g", 0, 0, 4096, "wg")
    wir = AR.alloc([2, 8, 2, 128], BF16)
    pA = AR.alloc([4104], F32)
    xbc = AR.alloc([NT], F32)
    xbb = AR.alloc([NT], BF16)
    Ib = AR.alloc([NT], F32)
    Bt = AR.alloc([NT], F32)
    Hf = AR.alloc([NT], F32)
    Gb = AR.alloc([NT], BF16)
    cl = AR.alloc([16], F32)
    P.op("act", I("activation", out=cl, in_=pvc("lam", 0, 16), func=AF.Exp, scale=-1.0), reads=["pv"], writes=["cl"])
    P.op("act", I("activation", out=cl, in_=cl, func=AF.Ln, bias=pvc("one")), reads=["pv"], writes=["cl"])
    P.op("dve", I("tensor_scalar", out=cl, in0=cl, scalar1=-4.0, scalar2=None, op0=ALU.mult), writes=["cl"])
    hga = AR.alloc([16], F32)
    hgx = AR.alloc([16], F32)
    P.op("dve", I("tensor_scalar", out=hga, in0=pvc("ga_b", 0, 16), scalar1=0.5, scalar2=None, op0=ALU.mult), reads=["pv"], writes=["hga"])
    P.op("dve", I("tensor_scalar", out=hgx, in0=pvc("gx_b", 0, 16), scalar1=0.5, scalar2=None, op0=ALU.mult), reads=["pv"], writes=["hgx"])
    quarter_ap = pvc("quarter")
    wir_v = wb["w_ir"].rearrange("p (k t n) -> p k t n", k=8, t=2)
    A = pA
    one_ap = pvc("one")
    SEG = 1024
    for c in range(8):
        ws = c % 2
        if c == 0:
            P.dma("sp", I("dma_start", out=wir[:, 0], in_=wir_v[:, :, :, 0:128]), reads=WKEYS[("w_ir", 0)], writes=[("wir", 0)])
        if c + 1 < 8:
            P.dma("sp", I("dma_start", out=wir[:, (c + 1) % 2], in_=wir_v[:, :, :, (c + 1) * 128:(c + 2) * 128]), reads=WKEYS[("w_ir", 0)], writes=[("wir", (c + 1) % 2)])
        Akeys = [("A", g) for g in range(4)]
        for i in range(8):
            sl = slice(i * 512, (i + 1) * 512)
            dst0 = 1 + i * 512 if i < 4 else 2052 + (i - 4) * 512
            b = rot.next()
            for kc in range(8):
                P.op("pe", I("matmul", PS[b], lhsT=wir[:, ws, kc, 1, :], rhs=hTa[:, kc, sl], start=(kc == 0), stop=(kc == 7)),
                     reads=[("wir", ws)], writes=[psk(b)], inc=(kc == 7))
            P.op("dve", I("tensor_copy", out=pA[:, dst0:dst0 + 512], in_=PS[b]), writes=[psk(b), "pA"] + Akeys)
        P.op("pool", I("memset", pA[:, 0:1], 0.0), writes=["pA"])
        P.op("pool", I("memset", pA[:, 4100:4102], 0.0), writes=["pA"])
        P.op("dve", I("tensor_scalar", out=pA[:, 2049:2051], in0=pA[:, 2052:2054], scalar1=keep_ap, scalar2=None, op0=ALU.mult), reads=["pv"], writes=["pA"])
        P.op("dve", I("tensor_scalar", out=pA[:, 2051:2052], in0=pA[:, 2048:2049], scalar1=keep_ap, scalar2=None, op0=ALU.mult), reads=["pv"], writes=["pA"])
        rcw = lambda k: pvc("rc_w", k * 8 + c)
        for sg in range(4):
            base = (0 if sg < 2 else 2051) + (sg % 2) * SEG
            o = slice(sg * SEG, (sg + 1) * SEG)
            P.op("act", I("activation", out=xbc[:, o], in_=pA[:, base + 1:base + 1 + SEG], func=AF.Identity, scale=rcw(1), bias=pvc("rc_b", c)),
                 reads=["pA", "pv"], writes=[("xbc", sg)])
        for sg in range(4):
            base = (0 if sg < 2 else 2051) + (sg % 2) * SEG
            o = slice(sg * SEG, (sg + 1) * SEG)
            for k in (0, 2, 3):
                P.op("dve", I("scalar_tensor_tensor", out=xbc[:, o], in0=pA[:, base + k:base + k + SEG], scalar=rcw(k), in1=xbc[:, o], op0=ALU.mult, op1=ALU.add),
                     reads=["pA", "pv"], writes=[("xbc", sg)] + ([("pAfree", sg)] if k == 3 else []))
            P.op("act", I("activation", out=xbb[:, o], in_=xbc[:, o], func=AF.Copy), reads=[("xbc", sg)], writes=[("xbb", sg)])
        pafree = [("pAfree", g) for g in range(4)]
        for e_ in range(2):
            segs = (0, 1, 2, 3) if e_ == 0 else (3, 2, 1, 0)
            hcl_e = cl[:, e_ * 8 + c:e_ * 8 + c + 1]
            for sg in segs:
                for i in (2 * sg, 2 * sg + 1):
                    sl = slice(i * 512, (i + 1) * 512)
                    b = rot.next()
                    P.op("pe", I("matmul", PS[b], lhsT=wg[:, 0, e_, c, :], rhs=xbb[:, sl], start=True, stop=True), reads=["wg", ("xbb", sg)], writes=[psk(b)])
                    P.op("act", I("activation", out=A[:, sl], in_=PS[b], func=AF.Tanh, scale=0.5, bias=hga[:, e_ * 8 + c:e_ * 8 + c + 1]), reads=["hga"] + pafree, writes=[psk(b), ("A", sg)])
                    b = rot.next()
                    P.op("pe", I("matmul", PS[b], lhsT=wg[:, 1, e_, c, :], rhs=xbb[:, sl], start=True, stop=True), reads=["wg", ("xbb", sg)], writes=[psk(b)])
                    P.op("act", I("activation", out=Ib[:, sl], in_=PS[b], func=AF.Tanh, scale=0.5, bias=hgx[:, e_ * 8 + c:e_ * 8 + c + 1]), reads=["hgx"], writes=[psk(b), ("Ib", sg)])
            for sg in segs:
                o = slice(sg * SEG, (sg + 1) * SEG)
                P.op("act", I("activation", out=A[:, o], in_=A[:, o], func=AF.Exp, scale=hcl_e, bias=hcl_e), reads=["cl"], writes=[("A", sg)])
                P.op("dve", I("scalar_tensor_tensor", out=Bt[:, o], in0=A[:, o], scalar=-1.0, in1=A[:, o], op0=ALU.mult, op1=ALU.mult), reads=[("A", sg)], writes=[("Bt", sg)])
            for sg in segs:
                o = slice(sg * SEG, (sg + 1) * SEG)
                P.op("act", I("activation", out=Bt[:, o], in_=Bt[:, o], func=AF.Sqrt, scale=0.25, bias=quarter_ap), reads=["pv"], writes=[("Bt", sg)])
                P.op("dve", I("scalar_tensor_tensor", out=Bt[:, o], in0=Ib[:, o], scalar=1.0, in1=Bt[:, o], op0=ALU.add, op1=ALU.mult), reads=[("Ib", sg)], writes=[("Bt", sg)])
                P.op("dve", I("scalar_tensor_tensor", out=Bt[:, o], in0=Bt[:, o], scalar=1.0, in1=xbc[:, o], op0=ALU.mult, op1=ALU.mult), reads=[("xbc", sg)], writes=[("Bt", sg)])
                if e_ == 0:
                    if sg == 2:
                        P.op("dve", I("tensor_scalar", out=A[:, 2048:2049], in0=A[:, 2048:2049], scalar1=keep_ap, scalar2=None, op0=ALU.mult), reads=["pv"], writes=[("A", sg)])
                    init = 0.0 if sg == 0 else Hf[:, sg * SEG - 1:sg * SEG]
                    P.op("dve", I("tensor_tensor_scan", out=Hf[:, o], data0=A[:, o], data1=Bt[:, o], initial=init, op0=ALU.mult, op1=ALU.add),
                         reads=[("A", sg), ("Bt", sg)] + ([("Hf", sg - 1)] if sg > 0 else []), writes=[("Hf", sg)])
                else:
                    if sg == 1:
                        P.op("dve", I("tensor_scalar", out=A[:, 2047:2048], in0=A[:, 2047:2048], scalar1=keep_ap, scalar2=None, op0=ALU.mult), reads=["pv"], writes=[("A", sg)])
                    lo, hi = sg * SEG, (sg + 1) * SEG
                    init = 0.0 if sg == 3 else Ib[:, hi:hi + 1]
                    rv = lambda ap, lo=lo, hi=hi: ap[:, lo:hi][:, ::-1]
                    P.op("dve", I("tensor_tensor_scan", out=rv(Ib), data0=rv(A), data1=rv(Bt), initial=init, op0=ALU.mult, op1=ALU.add),
                         reads=[("A", sg), ("Bt", sg)] + ([("Ib", sg + 1)] if sg < 3 else []), writes=[("Ib", sg)])
            if e_ == 0:
                for i in range(8):
                    sl = slice(i * 512, (i + 1) * 512)
                    b = rot.next()
                    for kc in range(8):
                        P.op("pe", I("matmul", PS[b], lhsT=wir[:, ws, kc, 0, :], rhs=hTa[:, kc, sl], start=(kc == 0), stop=(kc == 7)),
                             reads=[("wir", ws)], writes=[psk(b)], inc=(kc == 7))
                    P.op("act", I("activation", out=Gb[:, sl], in_=PS[b], func=AF.Gelu_apprx_tanh), writes=[psk(b), ("Gb", i // 2)])
        for sg in range(4):
            o = slice(sg * SEG, (sg + 1) * SEG)
            P.op("dve", I("scalar_tensor_tensor", out=Hf[:, o], in0=Hf[:, o], scalar=1.0, in1=Ib[:, o], op0=ALU.mult, op1=ALU.add), reads=[("Ib", sg)], writes=[("Hf", sg)])
            P.op("dve", I("scalar_tensor_tensor", out=xbb[:, o], in0=Hf[:, o], scalar=1.0, in1=Gb[:, o], op0=ALU.mult, op1=ALU.mult), reads=[("Hf", sg), ("Gb", sg)], writes=[("xbb", sg)])
        P.dma("sp", I("dma_start", out=yrec[c * 128:(c + 1) * 128, :], in_=xbb), reads=[("xbb", g) for g in range(4)], writes=[("yrec", c)])
    P.barrier()
    AR.pop()
    if stop_after == "B1":
        return finish(nc, P, AR, dbg, locals())

    fused_tiles(1, yrec, "w_or", xTb, None, True)
    return finish(nc, P, AR, dbg, locals())


def finish(nc, P, AR, dbg, loc):
    if dbg:
        for nm in dbg:
            src = loc[nm]
            shp = list(src.shape)
            o = nc.dram_tensor("dbg_" + nm, shp, F32, kind="ExternalOutput").ap()
            rows = shp[0]
            for r0 in range(0, rows, 128):
                r1 = min(rows, r0 + 128)
                P.dma("pool", I("dma_start", out=o[r0:r1], in_=src[r0:r1]), writes=[("dbg", nm, r0)])
    alld = []
    for e in P.E.values():
        if e.count > 0:
            alld.append(Dep(e.name + "_b", e.sem, e.count))
        for slot in e.ring:
            if slot[1] > 0:
                alld.append(DmaDep(e.name + "_b", slot[0], slot[1]))
    for e in P.E.values():
        waits = P._waits_for(e, alld)
        if waits:
            e.ops.append((waits, None, None, 0))
    P.emit()
    return nc


def kernel(**inputs):
    maps = host_prep(inputs)
    nc = build()
    res = run_bass_kernel_spmd(nc, maps, core_ids=list(range(8)))
    ys = [np.asarray(r["y"], np.float32) for r in res.results]
    y_prompt = np.stack(ys[0:4], 0).reshape(8, 2048, 1024)
    y_sample = np.stack(ys[4:8], 0)
    return (y_prompt, y_sample)
```
